# Optimizing a Trainium2 kernel written in Bass

```python
import jax, jax.numpy as jnp
from jax import lax
import numpy as np

D_MODEL = 1024
BATCH = 16
SEQ = 2048
DEPTH = 4

CHUNK = 64
N_EVEN = (DEPTH + 1) // 2
N_ODD = DEPTH // 2

A_WIDTH = D_MODEL // 2
A_HEAD_DIM = 128
A_HEADS = A_WIDTH // A_HEAD_DIM
A_SUB = 8
A_NSUB = CHUNK // A_SUB

B_WIDTH = D_MODEL // 2
B_HEAD_DIM = 64
B_HEADS = B_WIDTH // B_HEAD_DIM
B_DECAY_LORA = 64
B_ICL_LORA = 64
B_VRES_LORA = 32

C_WIDTH = D_MODEL
C_BLOCKS = 4
C_BLOCK_DIM = C_WIDTH // C_BLOCKS
C_CONV = 4
C_POW = 8.0

EVEN_SPLITS = (A_WIDTH, 2 * A_WIDTH, 3 * A_WIDTH, 4 * A_WIDTH, 4 * A_WIDTH + B_WIDTH, 4 * A_WIDTH + 2 * B_WIDTH, 4 * A_WIDTH + 3 * B_WIDTH, 4 * A_WIDTH + 4 * B_WIDTH)
EVEN_COLS = 4 * A_WIDTH + 5 * B_WIDTH
ODD_COLS = 2 * C_WIDTH

DEEPNORM_ALPHA = (2.0 * DEPTH) ** 0.25
DEEPNORM_BETA = (8.0 * DEPTH) ** -0.25
LN_EPS = 1e-5
RMS_EPS = 1e-6
B_GN_EPS = B_HEAD_DIM * 1e-5

kernel_name = 'hybrid_hgrn2_rwkv7_rglru_deepnorm_encoder'


def layer_norm(x, g, b):
    xf = x.astype(jnp.float32)
    mu = jnp.mean(xf, -1, keepdims=True)
    var = jnp.mean(jnp.square(xf - mu), -1, keepdims=True)
    return ((xf - mu) * lax.rsqrt(var + LN_EPS) * g.astype(jnp.float32) + b.astype(jnp.float32)).astype(x.dtype)


def token_shift(u):
    return jnp.pad(u, ((0, 0), (1, 0), (0, 0)))[:, :-1]


def hgrn2_chunkwise(q, f_pre, i, lb, norm_g):
    f32 = jnp.float32
    bsz, seq, _ = q.shape
    n_chunks = seq // CHUNK
    shp = (bsz, n_chunks, CHUNK, A_HEADS, A_HEAD_DIM)
    sb_shp = (bsz, n_chunks, A_NSUB, A_SUB, A_HEADS, A_HEAD_DIM)
    lb = lb.astype(f32)
    z = f_pre.astype(f32)
    log_f = jnp.log(lb + (1.0 - lb) * jax.nn.sigmoid(z)).reshape(shp)
    k = ((1.0 - lb) * jax.nn.sigmoid(-z)).reshape(shp)
    q = q.astype(f32).reshape(shp)
    i = i.astype(f32).reshape(shp)
    b = jnp.cumsum(log_f, axis=2)
    b_last = b[:, :, -1:]
    bs = b.reshape(sb_shp)
    qs = q.reshape(sb_shp)
    ks = k.reshape(sb_shp)
    i_s = i.reshape(sb_shp)
    pos = jnp.arange(A_SUB)
    diag_mask = (pos[:, None] >= pos[None, :])[:, :, None, None]
    rel = bs[:, :, :, :, None] - bs[:, :, :, None, :]
    dec = jnp.where(diag_mask, jnp.exp(jnp.minimum(rel, 0.0)), 0.0)
    s_diag = jnp.einsum('bcnthk,bcnshk,bcntshk->bcnhts', qs, ks, dec)
    o_diag = jnp.einsum('bcnhts,bcnshv->bcnthv', s_diag, i_s)
    b_start = (bs - log_f.reshape(sb_shp))[:, :, :, :1]
    q_off = qs * jnp.exp(bs - b_start)
    off_mask = ((jnp.arange(CHUNK) // A_SUB)[None, :] < jnp.arange(A_NSUB)[:, None])[:, :, None, None]
    k_off = jnp.where(off_mask, k[:, :, None] * jnp.exp(jnp.minimum(b_start - b[:, :, None], 0.0)), 0.0)
    s_off = jnp.einsum('bcnthk,bcnshk->bcnhts', q_off, k_off)
    o_off = jnp.einsum('bcnhts,bcshv->bcnthv', s_off, i)
    o_intra = (o_diag + o_off).reshape(shp)
    d_state = jnp.einsum('bcshk,bcshv->bchkv', k * jnp.exp(b_last - b), i)
    chunk_decay = jnp.exp(b_last[:, :, 0])

    def step(S, inp):
        dc, dS = inp
        return dc[..., None] * S + dS, S

    S0 = jnp.zeros((bsz, A_HEADS, A_HEAD_DIM, A_HEAD_DIM), f32)
    _, S_in = lax.scan(step, S0, (jnp.moveaxis(chunk_decay, 1, 0), jnp.moveaxis(d_state, 1, 0)))
    S_in = jnp.moveaxis(S_in, 0, 1)
    o_inter = jnp.einsum('bcthk,bchkv->bcthv', q * jnp.exp(b), S_in)
    o = (o_intra + o_inter).reshape(bsz, seq, A_HEADS, A_HEAD_DIM)
    o = o * lax.rsqrt(jnp.mean(o * o, -1, keepdims=True) + RMS_EPS)
    return o.reshape(bsz, seq, A_WIDTH) * norm_g.astype(f32)


def rwkv7_scan(r, w, k, v, a, b):
    bsz, _, n_heads, n = r.shape

    def step(S, inp):
        r_t, w_t, k_t, v_t, a_t, b_t = inp
        sa = jnp.einsum('bhvk,bhk->bhv', S, a_t)
        S = S * w_t[:, :, None, :] + sa[..., None] * b_t[:, :, None, :] + v_t[..., None] * k_t[:, :, None, :]
        return S, jnp.einsum('bhvk,bhk->bhv', S, r_t)

    xs = (jnp.moveaxis(r, 1, 0), jnp.moveaxis(w, 1, 0), jnp.moveaxis(k, 1, 0), jnp.moveaxis(v, 1, 0), jnp.moveaxis(a, 1, 0), jnp.moveaxis(b, 1, 0))
    S0 = jnp.zeros((bsz, n_heads, n, n), jnp.float32)
    _, y = lax.scan(step, S0, xs)
    return jnp.moveaxis(y, 0, 1)


def rwkv7_mix(rB, kB, vB, zB, mu, w0, w1, w2, a0, a1, a2, k_k, k_a, r_k, gn_g, gn_b, v_first, vres):
    f32 = jnp.float32
    bsz, seq, _ = rB.shape
    hn = (B_HEADS, B_HEAD_DIM)

    def heads(t):
        return t.astype(f32).reshape(bsz, seq, B_HEADS, B_HEAD_DIM)

    def lerp(u, m):
        return u + (token_shift(u) - u) * m

    r = lerp(rB, mu[0])
    k = lerp(kB, mu[1])
    v = lerp(vB, mu[2])
    z_delta = token_shift(zB) - zB
    zw = zB + z_delta * mu[3]
    za = zB + z_delta * mu[4]
    w_log = -jax.nn.softplus(-(w0 + jnp.tanh(zw @ w1) @ w2).astype(f32)) - 0.5
    decay = jnp.exp(-jnp.exp(w_log))
    if vres is None:
        v_first = v
    else:
        v_mu, v0, v1, v2 = vres
        zv = zB + z_delta * v_mu
        v = v + (v_first - v) * jax.nn.sigmoid(v0 + (zv @ v1) @ v2)
    icl = heads(jax.nn.sigmoid((a0 + (za @ a1) @ a2).astype(f32)))
    kk = heads(k * k_k)
    kk = kk * lax.rsqrt(jnp.maximum(jnp.sum(kk * kk, -1, keepdims=True), 1e-24))
    kh = heads(k) * (1.0 + (icl - 1.0) * k_a.astype(f32).reshape(hn))
    rh = heads(r)
    vh = heads(v)
    y = rwkv7_scan(rh, heads(decay), kh, vh, -kk, kk * icl)
    y_mu = jnp.mean(y, -1, keepdims=True)
    y_var = jnp.mean(jnp.square(y - y_mu), -1, keepdims=True)
    y = (y - y_mu) * lax.rsqrt(y_var + B_GN_EPS) * gn_g.astype(f32).reshape(hn) + gn_b.astype(f32).reshape(hn)
    y = y + jnp.sum(rh * kh * r_k.astype(f32), -1, keepdims=True) * vh
    return y.reshape(bsz, seq, B_WIDTH), v_first


def rglru_branch(xb, conv_w, conv_b, wa, ba, wx, bx, lam):
    f32 = jnp.float32
    bsz, seq, _ = xb.shape
    xc = lax.conv_general_dilated(xb.astype(f32), conv_w.astype(f32)[:, None, :], (1,), [(C_CONV - 1, 0)], dimension_numbers=('NWC', 'WIO', 'NWC'), feature_group_count=C_WIDTH) + conv_b.astype(f32)
    xh = xc.reshape(bsz, seq, C_BLOCKS, C_BLOCK_DIM)
    gate_r = jax.nn.sigmoid(jnp.einsum('btgi,gij->btgj', xh, wa.astype(f32)) + ba.astype(f32)).reshape(bsz, seq, C_WIDTH)
    gate_i = jax.nn.sigmoid(jnp.einsum('btgi,gij->btgj', xh, wx.astype(f32)) + bx.astype(f32)).reshape(bsz, seq, C_WIDTH)
    log_a = -C_POW * gate_r * jax.nn.softplus(-lam.astype(f32))
    a = jnp.exp(log_a)
    u = jnp.sqrt(-jnp.expm1(2.0 * log_a)) * (gate_i * xc)

    def combine(left, right):
        a_l, b_l = left
        a_r, b_r = right
        return a_l * a_r, a_r * b_l + b_r

    _, h = lax.associative_scan(combine, (a, u), axis=1)
    return h


def setup_inputs(seed: int = 0) -> dict:
    key = jax.random.key(seed)
    ks = iter(jax.random.split(key, 48))
    f32 = jnp.float32

    def nrm(shape, s):
        return jax.random.normal(next(ks), shape, f32) * s

    def uni(shape, lo, hi):
        return jax.random.uniform(next(ks), shape, f32, lo, hi)

    D = D_MODEL
    nv = N_EVEN - 1
    p_lam = uni((N_ODD, C_WIDTH), 0.9, 0.999) ** (1.0 / C_POW)
    return {
        'x': nrm((BATCH, SEQ, D), 1.0),
        'c': nrm((BATCH, D), 1.0),
        'ada_w': nrm((DEPTH, D, 3 * D), 0.1 * D ** -0.5),
        'ada_b': nrm((DEPTH, 3 * D), 0.01),
        'ln_g': 1.0 + nrm((DEPTH, D), 0.02),
        'ln_b': nrm((DEPTH, D), 0.02),
        'ev_w_in': nrm((N_EVEN, D, EVEN_COLS), D ** -0.5),
        'ev_w_out': nrm((N_EVEN, A_WIDTH + B_WIDTH, D), DEEPNORM_BETA * (A_WIDTH + B_WIDTH) ** -0.5),
        'a_lb_logits': nrm((N_EVEN, A_WIDTH), 0.5),
        'a_norm_g': 1.0 + nrm((N_EVEN, A_WIDTH), 0.02),
        'b_mu': uni((N_EVEN, 5, B_WIDTH), 0.0, 1.0),
        'b_w0': uni((N_EVEN, B_WIDTH), -6.0, -0.5),
        'b_w1': nrm((N_EVEN, B_WIDTH, B_DECAY_LORA), B_WIDTH ** -0.5),
        'b_w2': nrm((N_EVEN, B_DECAY_LORA, B_WIDTH), 0.1 * B_DECAY_LORA ** -0.5),
        'b_a0': nrm((N_EVEN, B_WIDTH), 0.1),
        'b_a1': nrm((N_EVEN, B_WIDTH, B_ICL_LORA), B_WIDTH ** -0.5),
        'b_a2': nrm((N_EVEN, B_ICL_LORA, B_WIDTH), 0.1 * B_ICL_LORA ** -0.5),
        'b_kk': 0.85 + nrm((N_EVEN, B_WIDTH), 0.02),
        'b_ka': 1.0 + nrm((N_EVEN, B_WIDTH), 0.02),
        'b_rk': nrm((N_EVEN, B_HEADS, B_HEAD_DIM), 0.1),
        'b_gn_g': 1.0 + nrm((N_EVEN, B_WIDTH), 0.02),
        'b_gn_b': nrm((N_EVEN, B_WIDTH), 0.02),
        'b_vmu': uni((nv, B_WIDTH), 0.0, 1.0),
        'b_v0': 1.0 + nrm((nv, B_WIDTH), 0.1),
        'b_v1': nrm((nv, B_WIDTH, B_VRES_LORA), B_WIDTH ** -0.5),
        'b_v2': nrm((nv, B_VRES_LORA, B_WIDTH), 0.1 * B_VRES_LORA ** -0.5),
        'od_w_in': nrm((N_ODD, D, ODD_COLS), D ** -0.5),
        'od_conv_w': nrm((N_ODD, C_CONV, C_WIDTH), C_CONV ** -0.5),
        'od_conv_b': nrm((N_ODD, C_WIDTH), 0.01),
        'od_wa': nrm((N_ODD, C_BLOCKS, C_BLOCK_DIM, C_BLOCK_DIM), C_BLOCK_DIM ** -0.5),
        'od_ba': nrm((N_ODD, C_BLOCKS, C_BLOCK_DIM), 0.01),
        'od_wx': nrm((N_ODD, C_BLOCKS, C_BLOCK_DIM, C_BLOCK_DIM), C_BLOCK_DIM ** -0.5),
        'od_bx': nrm((N_ODD, C_BLOCKS, C_BLOCK_DIM), 0.01),
        'od_lam': jnp.log(p_lam) - jnp.log1p(-p_lam),
        'od_w_out': nrm((N_ODD, C_WIDTH, D), DEEPNORM_BETA * C_WIDTH ** -0.5),
    }


def reference(x, c, ada_w, ada_b, ln_g, ln_b, ev_w_in, ev_w_out, a_lb_logits, a_norm_g, b_mu, b_w0, b_w1, b_w2, b_a0, b_a1, b_a2, b_kk, b_ka, b_rk, b_gn_g, b_gn_b, b_vmu, b_v0, b_v1, b_v2, od_w_in, od_conv_w, od_conv_b, od_wa, od_ba, od_wx, od_bx, od_lam, od_w_out):
    f32 = jnp.float32
    cond = jax.nn.silu(c)
    p_lb = jax.nn.softmax(a_lb_logits.astype(f32), axis=0)
    lbs = jnp.cumsum(p_lb, axis=0) - p_lb[0]
    v_first = None
    for l in range(DEPTH):
        mod = cond @ ada_w[l] + ada_b[l]
        shift, scale, gate = jnp.split(mod, 3, axis=-1)
        h = x * (1.0 + scale[:, None]) + shift[:, None]
        if l % 2 == 0:
            e = l // 2
            u = h @ ev_w_in[e]
            qA, fA, iA, gA, rB, kB, vB, zB, gB = jnp.split(u, EVEN_SPLITS, axis=-1)
            oA = hgrn2_chunkwise(qA, fA, iA, lbs[e], a_norm_g[e])
            vres = None if e == 0 else (b_vmu[e - 1], b_v0[e - 1], b_v1[e - 1], b_v2[e - 1])
            oB, vf = rwkv7_mix(rB, kB, vB, zB, b_mu[e], b_w0[e], b_w1[e], b_w2[e], b_a0[e], b_a1[e], b_a2[e], b_kk[e], b_ka[e], b_rk[e], b_gn_g[e], b_gn_b[e], v_first, vres)
            if e == 0:
                v_first = vf
            mixed = jnp.concatenate([oA * jax.nn.silu(gA.astype(f32)), oB * jax.nn.silu(gB.astype(f32))], axis=-1).astype(x.dtype)
            y = mixed @ ev_w_out[e]
        else:
            o = l // 2
            u = h @ od_w_in[o]
            xC, gC = jnp.split(u, 2, axis=-1)
            hC = rglru_branch(xC, od_conv_w[o], od_conv_b[o], od_wa[o], od_ba[o], od_wx[o], od_bx[o], od_lam[o])
            y = (hC * jax.nn.silu(gC.astype(f32))).astype(x.dtype) @ od_w_out[o]
        x = layer_norm(DEEPNORM_ALPHA * x + (1.0 + gate[:, None]) * y, ln_g[l], ln_b[l])
    return x
```

```python
import numpy as np
from contextlib import ExitStack
import concourse.bass as bass
import concourse.mybir as mybir
from concourse.bass_utils import run_bass_kernel_spmd

F32 = mybir.dt.float32
BF16 = mybir.dt.bfloat16
ALU = mybir.AluOpType
AF = mybir.ActivationFunctionType
AX = mybir.AxisListType

NCORE = 8
D = 1024
T = 2048
NB = 2
NTOK = NB * T
TT = 256
NTILE = T // TT
NCH = TT // 64
NTB = TT // 128
DEPTH = 4
ALPHA = (2.0 * DEPTH) ** 0.25
LN_EPS = 1e-5
RMS_EPS = 1e-6
GN_EPS = 64 * 1e-5
EVC = 4608
STG = 1152
EMBED_WAIT = True


class Buf:
    __slots__ = ("name", "last_w", "readers")

    def __init__(self, name):
        self.name = name
        self.last_w = None
        self.readers = {}


class Op:
    __slots__ = ("eng", "fn", "deps", "signaled", "value", "sem", "is_dma", "grp")

    def __init__(self, eng, fn, is_dma):
        self.eng = eng
        self.fn = fn
        self.deps = []
        self.signaled = False
        self.value = None
        self.sem = None
        self.is_dma = is_dma
        self.grp = None


ENGS = ("tensor", "vector", "scalar", "gpsimd", "sync")


class Sched:
    def __init__(self, nc):
        self.nc = nc
        self.ops = []
        self.bufs = {}
        self.last = {}
        self.rec = None

    def bf(self, name):
        b = self.bufs.get(name)
        if b is None:
            b = self.bufs[name] = Buf(name)
        return b

    def op(self, eng, fn, reads=(), writes=(), dma=False, grp=None, holder=None):
        if self.rec is not None:
            holder = [None]
            self.rec.append((eng, fn, tuple(reads), tuple(writes), dma, grp, holder))
            return holder
        o = Op(eng, fn, dma)
        if dma:
            o.grp = grp
            o.signaled = True
        deps = {}
        reads = [self.bf(r) for r in reads]
        writes = [self.bf(w) for w in writes]
        for r in reads:
            if r.last_w is not None:
                deps[id(r.last_w)] = (r.last_w, True)
        for w in writes:
            if w.last_w is not None:
                deps[id(w.last_w)] = (w.last_w, True)
            for rd in w.readers.values():
                deps.setdefault(id(rd), (rd, False))
        for p, raw in deps.values():
            if p is o:
                continue
            same = (p.eng == eng) and (not p.is_dma)
            if same and not dma and eng == "tensor":
                continue
            o.deps.append(p)
            p.signaled = True
        for r in reads:
            r.readers[("dma", id(o)) if dma else eng] = o
        for w in writes:
            w.last_w = o
            w.readers = {}
        self.ops.append(o)
        if not dma:
            self.last[eng] = o
        if holder is not None:
            holder[0] = o
        return o

    def record(self, f):
        outer = self.rec
        self.rec = []
        f()
        r, self.rec = self.rec, outer
        return r

    def replay(self, lst):
        if self.rec is not None:
            self.rec.extend(lst)
            return
        for eng, fn, reads, writes, dma, grp, holder in lst:
            self.op(eng, fn, reads, writes, dma=dma, grp=grp, holder=holder)

    def fence(self, fns):
        prev = dict(self.last)
        for eng, fn in fns.items():
            o = self.op(eng, fn)
            for pe, p in prev.items():
                if pe != eng and p not in o.deps:
                    o.deps.append(p)
                    p.signaled = True

    def dma(self, out, in_, group, reads=(), writes=(), eng="sync"):
        return self.op(eng, lambda e: e.dma_start(out=out, in_=in_), reads, writes, dma=True, grp=group)

    def emit(self, final_wait_ops=()):
        nc = self.nc
        with ExitStack() as es:
            sems = {e: es.enter_context(nc.semaphore("s_" + e)) for e in ENGS}
            dma_sems = {}
            cnt = {e: 0 for e in ENGS}
            dcnt = {}
            for o in self.ops:
                if o.is_dma:
                    key = o.grp
                    if key not in dma_sems:
                        dma_sems[key] = es.enter_context(nc.semaphore("d_" + str(key)))
                        dcnt[key] = 0
                    dcnt[key] += 16
                    o.sem = dma_sems[key]
                    o.value = dcnt[key]
                else:
                    if o.signaled:
                        cnt[o.eng] += 1
                        o.value = cnt[o.eng]
                    o.sem = sems[o.eng]
            finals = [p[0] if isinstance(p, list) else p for p in final_wait_ops]
            block = es.enter_context(nc.Block())
            per_eng = {e: [o for o in self.ops if o.eng == e] for e in ENGS}

            def make(ename):
                def body(eng):
                    waited = {}
                    for o in per_eng[ename]:
                        need = []
                        for p in o.deps:
                            k = id(p.sem)
                            if waited.get(k, 0) >= p.value:
                                continue
                            waited[k] = p.value
                            need = [q for q in need if q[0] is not p.sem] + [(p.sem, p.value)]
                        emb = None
                        if need and EMBED_WAIT and not o.is_dma:
                            emb = need.pop()
                        for sem_, val_ in need:
                            eng.wait_ge(sem_, val_)
                        ins = o.fn(eng)
                        if emb is not None:
                            ins._wait_ge(emb[0], emb[1])
                        if o.is_dma:
                            ins.then_inc(o.sem, 16)
                        elif o.signaled:
                            ins.then_inc(o.sem, 1)
                    if ename == "sync":
                        for p in finals:
                            eng.wait_ge(p.sem, p.value)
                return body

            for e in ENGS:
                if per_eng[e] or (e == "sync" and finals):
                    getattr(block, e)(make(e))
            self.stats = {e: len(per_eng[e]) for e in ENGS}


def pp_layout():
    lay = {}
    col = [0]

    def add(name, n):
        lay[name] = (col[0], n)
        col[0] += n

    for l in range(DEPTH):
        add(f"adab{l}", 24)
        add(f"lng{l}", 8)
        add(f"lnb{l}", 8)
    for e in range(2):
        add(f"lbl{e}", 4)
        add(f"ang{e}", 4)
        for i in range(5):
            add(f"mu{e}_{i}", 4)
        for nm in ("w0", "a0", "kk", "ka", "rk", "gng", "gnb"):
            add(f"{nm}{e}", 4)
    add("vmu", 4)
    add("v0", 4)
    for o in range(2):
        for j in range(4):
            add(f"cw{o}_{j}", 8)
        for nm in ("cb", "ba", "bx", "lam"):
            add(f"{nm}{o}", 8)
    return lay, col[0]


PP_LAY, NPP = pp_layout()


def pack_pp(inp):
    pp = np.zeros((128, NPP), np.float32)

    def put(name, v):
        c0, n = PP_LAY[name]
        pp[:, c0:c0 + n] = np.asarray(v, np.float32).reshape(n, 128).T

    for l in range(DEPTH):
        put(f"adab{l}", inp["ada_b"][l])
        put(f"lng{l}", inp["ln_g"][l])
        put(f"lnb{l}", inp["ln_b"][l])
    for e in range(2):
        put(f"lbl{e}", inp["a_lb_logits"][e])
        put(f"ang{e}", inp["a_norm_g"][e])
        for i in range(5):
            put(f"mu{e}_{i}", inp["b_mu"][e, i])
        put(f"w0{e}", inp["b_w0"][e])
        put(f"a0{e}", inp["b_a0"][e])
        put(f"kk{e}", inp["b_kk"][e])
        put(f"ka{e}", inp["b_ka"][e])
        put(f"rk{e}", inp["b_rk"][e].reshape(-1))
        put(f"gng{e}", inp["b_gn_g"][e])
        put(f"gnb{e}", inp["b_gn_b"][e])
    put("vmu", inp["b_vmu"][0])
    put("v0", inp["b_v0"][0])
    for o in range(2):
        for j in range(4):
            put(f"cw{o}_{j}", inp["od_conv_w"][o, j])
        put(f"cb{o}", inp["od_conv_b"][o])
        put(f"ba{o}", inp["od_ba"][o].reshape(-1))
        put(f"bx{o}", inp["od_bx"][o].reshape(-1))
        put(f"lam{o}", inp["od_lam"][o])
    return pp


WEIGHT_SPECS = [
    ("ada_w", [DEPTH, D, 3 * D]),
    ("ev_w_in", [2, D, EVC]),
    ("ev_w_out", [2, D, D]),
    ("b_w1", [2, 512, 64]), ("b_w2", [2, 64, 512]),
    ("b_a1", [2, 512, 64]), ("b_a2", [2, 64, 512]),
    ("b_v1", [1, 512, 32]), ("b_v2", [1, 32, 512]),
        ("od_w_in", [2, D, 2 * D]),
    ("od_wa", [2, 4, 256, 256]), ("od_wx", [2, 4, 256, 256]),
    ("od_w_out", [2, D, D]),
]


def build(layers=(0, 1, 2, 3), debug=False):
    nc = bass.Bass("TRN2", target_bir_lowering=False)
    S = Sched(nc)
    with ExitStack() as es:
        def din(name, shape):
            return nc.dram_tensor(name, shape, F32, kind="ExternalInput").ap()

        xT = din("xT", [D, NTOK])
        cTd = din("cT", [128, 16])
        ppd = din("pp", [128, NPP])
        W = {name: din(name, shape) for name, shape in WEIGHT_SPECS}
        outT = nc.dram_tensor("outT", [D, NTOK], F32, kind="ExternalOutput").ap()
        scr = [nc.dram_tensor(f"scr{i}", [D, NTOK], F32).ap() for i in range(2)]
        vfd = nc.dram_tensor("vfd", [512, NTOK], F32).ap()
        dbg = nc.dram_tensor("dbgT", [D, NTOK], BF16, kind="ExternalOutput").ap() if debug else None

        def sb(name, shape, dt=F32):
            return es.enter_context(nc.sbuf_tensor(name, shape, dt))

        pbanks = [es.enter_context(nc.psum_tensor(f"pb{i}", [128, 512], F32)) for i in range(8)]
        pools = {"hps": [0], "hacc": [1], "rproj": [2], "rmm": [3, 4], "racc": [5], "eproj": [6], "stat": [7],
                 "o0": [0, 4], "o1": [1, 5], "o2": [2], "o3": [3], "mm": [3, 4]}
        prr = {k: 0 for k in pools}

        def ps(tag):
            lst = pools[tag]
            i = lst[prr[tag] % len(lst)]
            prr[tag] += 1
            return pbanks[i], f"pb{i}"

        def V(fn, r=(), w=()):
            return S.op("vector", fn, r, w)

        def A(fn, r=(), w=()):
            return S.op("scalar", fn, r, w)

        def G(fn, r=(), w=()):
            return S.op("gpsimd", fn, r, w)

        def PE(fn, r=(), w=()):
            return S.op("tensor", fn, r, w)

        def act(out, in_, func, r, w, scale=1.0, bias=None):
            if bias is None:
                return A(lambda e: e.activation(out=out, in_=in_, func=func, scale=scale), r, w)
            return A(lambda e: e.activation(out=out, in_=in_, func=func, scale=scale, bias=bias), r, w)

        def mm(out, lhsT, rhs, r, w, start=True, stop=True):
            return PE(lambda e: e.matmul(out, lhsT=lhsT, rhs=rhs, start=start, stop=stop), r, w)

        ppt = sb("ppt", [128, NPP])
        S.dma(ppt[:], ppd, "ppt", writes=["ppt"])

        def PP(name, j=None):
            c0, n = PP_LAY[name]
            return ppt[:, c0:c0 + n] if j is None else ppt[:, c0 + j:c0 + j + 1]

        identf = sb("identf", [128, 128])
        identb = sb("identb", [128, 128], BF16)
        ones = sb("ones", [128, 128])
        bo = sb("bo", [128, 128])
        onesb = sb("onesb", [128, 128], BF16)
        bob = sb("bob", [128, 128], BF16)
        bo2 = sb("bo2", [128, 2])
        rm = sb("rm", [128, TT])
        mHf = sb("mHf", [128, 128])
        mk64 = sb("mk64", [64, 3, 64])
        mask512 = sb("mask512", [64, 8, 64], BF16)
        mL2 = sb("mL2", [64, 2, 64], BF16)
        id8 = sb("id8", [64, 8, 64], BF16)
        epsc = sb("epsc", [128, 6])
        G(lambda e: e.memset(identf[:], 1.0), w=["identf"])
        G(lambda e: e.affine_select(out=identf[:], in_=identf[:], pattern=[[-1, 128]], compare_op=ALU.is_equal,
                                    fill=0.0, base=0, channel_multiplier=1), w=["identf"])
        V(lambda e: e.tensor_copy(out=identb[:], in_=identf[:]), r=["identf"], w=["identb"])
        G(lambda e: e.memset(ones[:], 1.0), w=["ones"])
        G(lambda e: e.memset(bo[:], 0.0), w=["bo"])
        G(lambda e: e.memset(bo[0:64, 0:64], 1.0), w=["bo"])
        G(lambda e: e.memset(bo[64:128, 64:128], 1.0), w=["bo"])
        V(lambda e: e.tensor_copy(out=onesb[:], in_=ones[:]), r=["ones"], w=["onesb"])
        V(lambda e: e.tensor_copy(out=bob[:], in_=bo[:]), r=["bo"], w=["bob"])
        G(lambda e: e.memset(bo2[:], 0.0), w=["bo2"])
        G(lambda e: e.memset(bo2[0:64, 0:1], 1.0), w=["bo2"])
        G(lambda e: e.memset(bo2[64:128, 1:2], 1.0), w=["bo2"])
        G(lambda e: e.memset(rm[:], 1.0), w=["rm"])
        G(lambda e: e.memset(rm[:].rearrange("p (c t) -> p c t", t=64)[:, :, 0:1], 0.0), w=["rm"])
        G(lambda e: e.memset(mHf[:], 1.0), w=["mHf"])
        G(lambda e: e.affine_select(out=mHf[:], in_=mHf[:], pattern=[[1, 128]], compare_op=ALU.is_ge,
                                    fill=0.0, base=0, channel_multiplier=-1), w=["mHf"])
        G(lambda e: e.memset(mHf[0:64, 64:128], 0.0), w=["mHf"])
        G(lambda e: e.memset(mk64[:], 1.0), w=["mk64"])
        G(lambda e: e.affine_select(out=mk64[:, 0, :], in_=mk64[:, 0, :], pattern=[[1, 64]], compare_op=ALU.is_ge,
                                    fill=0.0, base=0, channel_multiplier=-1), w=["mk64"])
        G(lambda e: e.affine_select(out=mk64[:, 1, :], in_=mk64[:, 1, :], pattern=[[1, 64]], compare_op=ALU.is_ge,
                                    fill=0.0, base=-1, channel_multiplier=-1), w=["mk64"])
        G(lambda e: e.affine_select(out=mk64[:, 2, :], in_=mk64[:, 2, :], pattern=[[-1, 64]], compare_op=ALU.is_ge,
                                    fill=0.0, base=-1, channel_multiplier=1), w=["mk64"])
        for q in range(8):
            kind = 1 if (q % 2 == 0) else 0
            V(lambda e, q=q, kind=kind: e.tensor_copy(out=mask512[:, q, :], in_=mk64[:, kind, :]), r=["mk64"], w=["mask512"])
            V(lambda e, q=q: e.tensor_copy(out=id8[:, q, :], in_=identf[0:64, 0:64]), r=["identf"], w=["id8"])
        for h in range(2):
            V(lambda e, h=h: e.tensor_copy(out=mL2[:, h, :], in_=mk64[:, 2, :]), r=["mk64"], w=["mL2"])
        G(lambda e: e.memset(epsc[:, 0:1], LN_EPS / (ALPHA * ALPHA)), w=["epsc"])
        G(lambda e: e.memset(epsc[:, 1:2], RMS_EPS), w=["epsc"])
        G(lambda e: e.memset(epsc[:, 2:3], GN_EPS), w=["epsc"])
        G(lambda e: e.memset(epsc[:, 3:4], 1.0), w=["epsc"])
        G(lambda e: e.memset(epsc[:, 4:5], -1.0), w=["epsc"])
        G(lambda e: e.memset(epsc[:, 5:6], 1e-18), w=["epsc"])

        ct = sb("ct", [128, 16])
        cond = sb("cond", [128, 16])
        S.dma(ct[:], cTd, "ct", writes=["ct"])
        act(cond[:], ct[:], AF.Exp, ["ct"], ["cond"], scale=-1.0)
        act(cond[:], cond[:], AF.Ln, ["epsc"], ["cond"], bias=epsc[:, 3:4])
        act(cond[:], cond[:], AF.Exp, [], ["cond"], scale=-1.0)
        V(lambda e: e.tensor_tensor(out=cond[:], in0=cond[:], in1=ct[:], op=ALU.mult), r=["ct"], w=["cond"])

        stage = [sb(f"wst{i}", [128, STG]) for i in range(2)]
        stg_i = [0]

        def stage_load(src, parts, n, view=None):
            i = stg_i[0] % 2
            stg_i[0] += 1
            dst = stage[i][0:parts, 0:n] if view is None else view(stage[i])
            S.dma(dst, src, f"wst{i}", writes=[f"wst{i}"])
            return stage[i], f"wst{i}"

        def cast(eng, out, in_, r, w):
            if eng == "scalar":
                return A(lambda e: e.activation(out=out, in_=in_, func=AF.Copy), r, w)
            return S.op(eng, lambda e: e.tensor_copy(out=out, in_=in_), r, w)

        CE = ("gpsimd", "vector", "scalar")

        modt = sb("modt", [128, DEPTH, 24, 2])
        ms = sb("ms", [128, 2, TT])
        mrow = ms[0:2, :, :].rearrange("p a n -> p (a n)")
        for l in layers:
            mp, mpn = pbanks[0], "pb0"
            mpv = mp[:, 0:48].rearrange("p (j b) -> p j b", b=2)
            awl = W["ada_w"][l].rearrange("(dc p) n -> p dc n", p=128)
            for cgp in range(6):
                pm, pmn = ps("mm")
                for dp in range(4):
                    st, stn = stage_load(awl[:, 2 * dp:2 * dp + 2, cgp * 512:(cgp + 1) * 512], 128, 1024,
                                         view=lambda s_: s_[:, 0:1024].rearrange("p (dc n) -> p dc n", dc=2))
                    sv = st[:, 0:1024].rearrange("p (dc n) -> p dc n", dc=2)
                    for i_ in range(2):
                        dc = 2 * dp + i_
                        mm(pm[0:2, 0:512], cond[:, dc * 2:dc * 2 + 2], sv[:, i_, :], [stn, "cond"], [pmn], start=(dc == 0), stop=(dc == 7))
                act(mrow, pm[0:2, 0:512], AF.Copy, [], [pmn, "mrow"])
                for k_ in range(4):
                    jc = cgp * 4 + k_
                    PE(lambda e, jc=jc, k_=k_, mpv=mpv: e.transpose(out=mpv[:, jc, :], in_=mrow[0:2, k_ * 128:(k_ + 1) * 128], identity=identf[0:2, 0:2]),
                       ["mrow", "identf"], [mpn])
            V(lambda e, l=l, mpv=mpv: e.tensor_tensor(out=modt[:, l], in0=mpv,
                                                      in1=PP(f"adab{l}").unsqueeze(2).to_broadcast([128, 24, 2]), op=ALU.add),
              r=["ppt"], w=[mpn, "modt"])
            V(lambda e, l=l: e.tensor_scalar_add(out=modt[:, l, 8:16, :], in0=modt[:, l, 8:16, :], scalar1=1.0), w=["modt"])
            V(lambda e, l=l: e.tensor_scalar(out=modt[:, l, 16:24, :], in0=modt[:, l, 16:24, :], scalar1=1.0, scalar2=1.0 / ALPHA,
                                             op0=ALU.add, op1=ALU.mult), w=["modt"])

        lbt = sb("lbt", [128, 2, 4])
        omlt = sb("omlt", [128, 2, 4])
        nomlt = sb("nomlt", [128, 2, 4])
        tq = sb("tq", [128, 8, 4])
        l0, l1 = PP("lbl0"), PP("lbl1")
        V(lambda e: e.tensor_tensor(out=tq[:, 0], in0=l0, in1=l1, op=ALU.max), r=["ppt"], w=["tq"])
        V(lambda e: e.tensor_tensor(out=tq[:, 1], in0=l0, in1=tq[:, 0], op=ALU.subtract), r=["ppt", "tq"], w=["tq"])
        V(lambda e: e.tensor_tensor(out=tq[:, 2], in0=l1, in1=tq[:, 0], op=ALU.subtract), r=["ppt", "tq"], w=["tq"])
        act(tq[:, 1:3], tq[:, 1:3], AF.Exp, ["tq"], ["tq"])
        V(lambda e: e.tensor_tensor(out=tq[:, 3], in0=tq[:, 1], in1=tq[:, 2], op=ALU.add), r=["tq"], w=["tq"])
        act(tq[:, 3], tq[:, 3], AF.Ln, [], ["tq"])
        act(tq[:, 3], tq[:, 3], AF.Exp, [], ["tq"], scale=-1.0)
        V(lambda e: e.tensor_tensor(out=tq[:, 4], in0=tq[:, 1], in1=tq[:, 3], op=ALU.mult), r=["tq"], w=["tq"])
        V(lambda e: e.tensor_tensor(out=tq[:, 5], in0=tq[:, 2], in1=tq[:, 3], op=ALU.mult), r=["tq"], w=["tq"])
        V(lambda e: e.tensor_tensor(out=lbt[:, 0], in0=tq[:, 4], in1=tq[:, 4], op=ALU.subtract), r=["tq"], w=["lbt"])
        V(lambda e: e.tensor_tensor(out=tq[:, 6], in0=tq[:, 4], in1=tq[:, 5], op=ALU.add), r=["tq"], w=["tq"])
        V(lambda e: e.tensor_tensor(out=lbt[:, 1], in0=tq[:, 6], in1=tq[:, 4], op=ALU.subtract), r=["tq"], w=["lbt"])
        V(lambda e: e.tensor_scalar(out=omlt[:], in0=lbt[:], scalar1=-1.0, scalar2=1.0, op0=ALU.mult, op1=ALU.add), r=["lbt"], w=["omlt"])
        V(lambda e: e.tensor_scalar(out=nomlt[:], in0=lbt[:], scalar1=1.0, scalar2=-1.0, op0=ALU.mult, op1=ALU.add), r=["lbt"], w=["nomlt"])

        c8t = sb("c8t", [128, 2, 8])
        tl = sb("tl", [128, 6, 8])
        for o in range(2):
            lam = PP(f"lam{o}")
            act(tl[:, 0], lam, AF.Exp, ["ppt"], ["tl"], scale=-1.0)
            V(lambda e: e.tensor_scalar(out=tl[:, 1], in0=tl[:, 0], scalar1=-0.2, scalar2=0.25, op0=ALU.mult, op1=ALU.add), r=["tl"], w=["tl"])
            for cst in (1.0 / 3.0, 0.5, 1.0):
                V(lambda e: e.tensor_tensor(out=tl[:, 1], in0=tl[:, 1], in1=tl[:, 0], op=ALU.mult), r=["tl"], w=["tl"])
                V(lambda e, cst=cst: e.tensor_scalar(out=tl[:, 1], in0=tl[:, 1], scalar1=-1.0, scalar2=cst, op0=ALU.mult, op1=ALU.add), r=["tl"], w=["tl"])
            V(lambda e: e.tensor_tensor(out=tl[:, 1], in0=tl[:, 1], in1=tl[:, 0], op=ALU.mult), r=["tl"], w=["tl"])
            act(tl[:, 2], tl[:, 0], AF.Ln, ["tl", "epsc"], ["tl"], bias=epsc[:, 3:4])
            V(lambda e: e.tensor_single_scalar(out=tl[:, 3], in_=tl[:, 0], scalar=0.05, op=ALU.is_lt), r=["tl"], w=["tl"])
            V(lambda e: e.tensor_tensor(out=tl[:, 4], in0=tl[:, 1], in1=tl[:, 2], op=ALU.subtract), r=["tl"], w=["tl"])
            V(lambda e: e.tensor_tensor(out=tl[:, 4], in0=tl[:, 4], in1=tl[:, 3], op=ALU.mult), r=["tl"], w=["tl"])
            V(lambda e: e.tensor_tensor(out=tl[:, 4], in0=tl[:, 4], in1=tl[:, 2], op=ALU.add), r=["tl"], w=["tl"])
            V(lambda e, o=o: e.tensor_scalar_mul(out=c8t[:, o], in0=tl[:, 4], scalar1=-8.0), r=["tl"], w=["c8t"])
        omka = sb("omka", [128, 2, 4])
        for e_ in range(2):
            V(lambda e, e_=e_: e.tensor_scalar(out=omka[:, e_], in0=PP(f"ka{e_}"), scalar1=-1.0, scalar2=1.0, op0=ALU.mult, op1=ALU.add),
              r=["ppt"], w=["omka"])

        NEG = {}
        ncol = 0
        for nm, n_ in (("w00", 4), ("w01", 4), ("a00", 4), ("a01", 4), ("v0", 4), ("ba0", 8), ("ba1", 8), ("bx0", 8), ("bx1", 8)):
            NEG[nm] = (ncol, n_)
            ncol += n_
        negb = sb("negb", [128, ncol])
        for nm, (c0_, n_) in NEG.items():
            V(lambda e, nm=nm, c0_=c0_, n_=n_: e.tensor_scalar_mul(out=negb[:, c0_:c0_ + n_], in0=PP(nm), scalar1=-1.0), r=["ppt"], w=["negb"])

        def NB_(nm, j):
            c0_, _ = NEG[nm]
            return negb[:, c0_ + j:c0_ + j + 1]

        wbig = sb("wbig", [128, 8 * EVC], BF16)
        woutb = sb("woutb", [128, 8, D], BF16)
        xts = [sb(f"xt{i}", [128, 8, TT]) for i in range(2)]
        hb = sb("hb", [128, 8, TT], BF16)
        mixeds = [sb(f"mixed{i}", [128, 8, TT], BF16) for i in range(2)]
        mut = ms[:, 0, :]
        sqt = [sb("sqt0", [128, TT])[:], ms[:, 1, :]]
        rst = sb("rst", [128, TT])

        vxo = sb("vxo", [128, TT])
        vft = sb("vft", [128, TT])
        lora = sb("lora", [128, 2176], BF16)
        fsc = sb("fsc", [128, 4])
        RWORDS = 15990
        Rr = sb("Rr", [128, RWORDS])
        roff = [0]

        def carve(shape, dt=F32, parts=128):
            n = 1
            for d_ in shape[1:]:
                n *= d_
            words = n if dt == F32 else (n + 1) // 2
            a = roff[0]
            roff[0] += words
            assert roff[0] <= RWORDS, f"region overflow {roff[0]} > {RWORDS}"
            ap = Rr[0:shape[0], a:a + words]
            if dt != F32:
                ap = ap.bitcast(dt)
            if len(shape) > 2:
                names = " ".join(f"d{i}" for i in range(1, len(shape)))
                kw = {f"d{i}": shape[i] for i in range(1, len(shape))}
                ap = ap.rearrange(f"p ({names}) -> p {names}", **kw)
            return ap

        import types
        c = types.SimpleNamespace(**dict(locals()))
        c.final_ops = []
        emit_layers(c)
        S.emit(final_wait_ops=c.final_ops)
    return nc, S


def _w(el):
    return len(el) if isinstance(el, list) else 1


DUR = {"tensor": 0.16, "vector": 0.42, "scalar": 0.40, "gpsimd": 0.35, "sync": 2.0}
XLAT = 0.45
SLAT = 0.20


def merge_threads(lists, prio=None):
    if prio is None:
        prio = [0.0] * len(lists)
    prio = [p for p, l in zip(prio, lists) if l]
    lists = [l for l in lists if l]
    if len(lists) <= 1:
        return flat(lists[0]) if lists else []
    eng_free = {}
    last_w = {}
    readers = {}
    out = []
    pos = [0] * len(lists)

    def est_start(op):
        eng, _, reads, writes = op[0], op[1], op[2], op[3]
        t = eng_free.get(eng, 0.0)
        for b in reads:
            w = last_w.get(b)
            if w is not None:
                t = max(t, w[0] + (XLAT if w[1] != eng else SLAT))
        for b in writes:
            w = last_w.get(b)
            if w is not None:
                t = max(t, w[0] + (XLAT if w[1] != eng else (0.0 if eng == "tensor" else SLAT)))
            for re_, rf in readers.get(b, {}).items():
                if re_ != eng:
                    t = max(t, rf + XLAT)
        return t

    def commit(op):
        eng, reads, writes = op[0], op[2], op[3]
        s = est_start(op)
        f = s + DUR.get(eng, 0.4)
        eng_free[eng] = f if eng != "sync" else s + 0.05
        for b in reads:
            readers.setdefault(b, {})[eng] = f
        for b in writes:
            last_w[b] = (f, eng)
            readers[b] = {}
        out.append(op)

    n_el = sum(len(l) for l in lists)
    k = 0
    while k < n_el:
        best, bt = None, None
        for i, l in enumerate(lists):
            if pos[i] < len(l):
                el = l[pos[i]]
                t = est_start(el[0] if isinstance(el, list) else el) - prio[i]
                if bt is None or t < bt - 1e-9:
                    best, bt = i, t
        el = lists[best][pos[best]]
        pos[best] += 1
        k += 1
        if isinstance(el, list):
            for op in el:
                commit(op)
        else:
            commit(el)
    return out


def flat(lst):
    out = []
    for el in lst:
        if isinstance(el, list):
            out.extend(el)
        else:
            out.append(el)
    return out


def types_ns():
    import types
    return types.SimpleNamespace()


def emit_layers(c):
    S, nc = c.S, c.nc
    V, A, G, PE, act, mm, ps, sb, PP = c.V, c.A, c.G, c.PE, c.act, c.mm, c.ps, c.sb, c.PP
    hb, xts, modt, wbig, woutb = c.hb, c.xts, c.modt, c.wbig, c.woutb
    layers = list(c.layers)
    ONE, MONE, TINY = c.epsc[:, 3:4], c.epsc[:, 4:5], c.epsc[:, 5:6]

    def sigm(buf, bn, src, r, w_extra=(), nbias=None, xscale=1.0):
        if nbias is None:
            act(buf, src, AF.Exp, r, list(w_extra) + [bn], scale=-xscale)
        else:
            act(buf, src, AF.Exp, list(r) + ["negb"], list(w_extra) + [bn], scale=-xscale, bias=nbias)
        act(buf, buf, AF.Ln, ["epsc"], [bn], bias=ONE[0:buf.shape[0], :])
        act(buf, buf, AF.Exp, [], [bn], scale=-1.0)
    c.sigm = sigm

    def xsrc_dst(li):
        src = c.xT if li == 0 else c.scr[(li - 1) % 2]
        dst = c.outT if li == len(layers) - 1 else c.scr[li % 2]
        return src, dst

    def load_x(li, b, ti, slot):
        src, _ = xsrc_dst(li)
        tok0 = b * T + ti * TT
        S.dma(xts[slot][:], src.rearrange("(dc p) n -> p dc n", p=128)[:, :, tok0:tok0 + TT], f"xt{slot}",
              reads=[f"dram{li}.{b}.{ti}"], writes=[f"xt{slot}.{dc}" for dc in range(8)])

    def load_weights(l):
        e_ = l // 2
        if l % 2 == 0:
            for dc in range(8):
                for pc in range(4):
                    st, stn = c.stage_load(c.W["ev_w_in"][e_, dc * 128:(dc + 1) * 128, pc * 1152:(pc + 1) * 1152], 128, 1152)
                    c.cast(c.CE[dc % 3], wbig[:, dc * EVC + pc * 1152: dc * EVC + (pc + 1) * 1152], st[:, 0:1152], [stn], [f"wbig.{dc}"])
            wo = c.W["ev_w_out"][e_]
        else:
            for dc in range(8):
                for pc in range(2):
                    st, stn = c.stage_load(c.W["od_w_in"][e_, dc * 128:(dc + 1) * 128, pc * 1024:(pc + 1) * 1024], 128, 1024)
                    c.cast(c.CE[dc % 3], wbig[:, dc * 2048 + pc * 1024:dc * 2048 + (pc + 1) * 1024], st[:, 0:1024], [stn], [f"wbig.{dc}"])
            for wi, wn in enumerate(("od_wa", "od_wx")):
                for g in range(4):
                    st, stn = c.stage_load(c.W[wn][e_, g].rearrange("(kc p) n -> p kc n", p=128), 128, 512,
                                           view=lambda s: s[:, 0:512].rearrange("p (kc n) -> p kc n", kc=2))
                    off = 16384 + wi * 2048 + g * 512
                    c.cast(c.CE[g % 3], wbig[:, off:off + 512], st[:, 0:512], [stn], ["wgate"])
            wo = c.W["od_w_out"][e_]
        for cc in range(8):
            st, stn = c.stage_load(wo[cc * 128:(cc + 1) * 128, :], 128, 1024)
            c.cast(c.CE[cc % 3], woutb[:, cc, :], st[:, 0:1024], [stn], [f"wout.{cc}"])

    def proj_fm(wv, col0, pool):
        pa, pn = ps(pool)
        for dc in range(8):
            mm(pa[:, 0:TT], wv[:, dc, col0:col0 + 128], hb[:, dc, :], [f"wbig.{dc}", f"hb.{dc}"], [pn],
               start=(dc == 0), stop=(dc == 7))
        return pa[:, 0:TT], pn
    c.proj_fm = proj_fm

    def tile_prologue(l, b, xt, slot):
        for dc in range(8):
            eng = "vector"
            S.op(eng, lambda e, dc=dc: e.tensor_scalar(out=hb[:, dc, :], in0=xt[:, dc, :],
                                                       scalar1=modt[:, l, 8 + dc, b:b + 1], scalar2=modt[:, l, dc, b:b + 1],
                                                       op0=ALU.mult, op1=ALU.add),
                 [f"xt{slot}.{dc}", "modt"], [f"hb.{dc}"])

    def tile_epilogue(li, l, b, ti, xt, slot, mixed, mxn):
        if c.debug and li == len(layers) - 1:
            tok0_ = b * T + ti * TT
            S.dma(c.dbg.rearrange("(dc p) n -> p dc n", p=128)[:, :, tok0_:tok0_ + TT], mixed[:], "dbgst",
                  reads=[f"{mxn}.{cc}" for cc in range(8)], writes=["dbgdram"])
        st_p, st_n = ps("stat")
        for dc in range(8):
            pa, pn = ps("eproj")
            for cc in range(8):
                mm(pa[:, 0:TT], woutb[:, cc, dc * 128:(dc + 1) * 128], mixed[:, cc, :], [f"wout.{cc}", f"{mxn}.{cc}"], [pn],
                   start=(cc == 0), stop=(cc == 7))
            V(lambda e, dc=dc, pa=pa: e.scalar_tensor_tensor(out=xt[:, dc, :], in0=pa[:, 0:TT], scalar=modt[:, l, 16 + dc, b:b + 1],
                                                            in1=xt[:, dc, :], op0=ALU.mult, op1=ALU.add),
              r=["modt"], w=[pn, f"xt{slot}.{dc}"])
            if dc == 0:
                act(c.sqt[1], xt[:, 0, :], AF.Square, [f"xt{slot}.0"], ["sqt1"])
            else:
                act(c.sqt[0], xt[:, dc, :], AF.Square, [f"xt{slot}.{dc}"], ["sqt0"])
                G(lambda e: e.tensor_tensor(out=c.sqt[1], in0=c.sqt[1], in1=c.sqt[0], op=ALU.add), r=["sqt0"], w=["sqt1"])
                if dc == 1:
                    G(lambda e: e.tensor_tensor(out=c.mut, in0=xt[:, 0, :], in1=xt[:, 1, :], op=ALU.add), r=[f"xt{slot}.0", f"xt{slot}.1"], w=["mut"])
                else:
                    G(lambda e, dc=dc: e.tensor_tensor(out=c.mut, in0=c.mut, in1=xt[:, dc, :], op=ALU.add), r=[f"xt{slot}.{dc}"], w=["mut"])
        mm(st_p[:, 0:2 * TT], c.ones[:], c.ms[:].rearrange("p a n -> p (a n)"), ["ones", "mut", "sqt1"], [st_n])
        mut, rst = c.mut, c.rst
        act(mut, st_p[:, 0:TT], AF.Copy, [], [st_n, "mut"], scale=1.0 / D)
        act(rst[:], mut, AF.Square, ["mut"], ["rst"])
        V(lambda e: e.scalar_tensor_tensor(out=rst[:], in0=st_p[:, TT:2 * TT], scalar=1.0 / D, in1=rst[:], op0=ALU.mult, op1=ALU.subtract),
          w=[st_n, "rst"])
        act(rst[:], rst[:], AF.Ln, ["epsc"], ["rst"], bias=c.epsc[:, 0:1])
        act(rst[:], rst[:], AF.Exp, [], ["rst"], scale=-0.5)
        for dc in range(8):
            V(lambda e, dc=dc: e.tensor_tensor(out=xt[:, dc, :], in0=xt[:, dc, :], in1=mut, op=ALU.subtract), r=["mut"], w=[f"xt{slot}.{dc}"])
        for dc in range(8):
            V(lambda e, dc=dc: e.tensor_tensor(out=xt[:, dc, :], in0=xt[:, dc, :], in1=rst[:], op=ALU.mult), r=["rst"], w=[f"xt{slot}.{dc}"])
        for dc in range(8):
            G(lambda e, dc=dc: e.tensor_scalar(out=xt[:, dc, :], in0=xt[:, dc, :], scalar1=PP(f"lng{l}", dc), scalar2=PP(f"lnb{l}", dc),
                                               op0=ALU.mult, op1=ALU.add), r=["ppt"], w=[f"xt{slot}.{dc}"])
        _, dst = xsrc_dst(li)
        tok0 = b * T + ti * TT
        o = S.dma(dst.rearrange("(dc p) n -> p dc n", p=128)[:, :, tok0:tok0 + TT], xt[:], f"xst{slot}",
                  reads=[f"xt{slot}.{dc}" for dc in range(8)], writes=[f"dram{li + 1}.{b}.{ti}"])
        if li == len(layers) - 1:
            c.final_ops.append(o)

    od = types_ns()

    def alloc_odd():
        c.roff[0] = 0
        od.xcr = c.carve([128, 8, TT + 3])
        od.xc = c.carve([128, 8, TT])
        od.xcb = c.carve([128, 8, TT], BF16)
        od.sgc = c.carve([128, 8, TT])
        od.gr = c.carve([128, 8, TT])
        od.gi = c.carve([128, 8, TT])
        od.at = c.carve([128, 8, TT])
        od.hc = [c.carve([128, TT]) for i in range(2)]
        od.hst = c.carve([128, 8])
        print("odd region words", c.roff[0])

    def odd_tile(l, b, ti, xt, slot, mixed, mxn, g):
        o_ = l // 2
        pool = f"o{g}"
        CS = (2 * g, 2 * g + 1)
        wv = wbig[:, 0:16384].rearrange("p (dc n) -> p dc n", dc=8)
        wg = wbig[:, 16384:16384 + 4096].rearrange("p (w g kc n) -> p w g kc n", w=2, g=4, kc=2)
        xcr, xc, xcb, sgc, gr, gi, at, hst = od.xcr, od.xc, od.xcb, od.sgc, od.gr, od.gi, od.at, od.hst
        hn = f"hst.{g}"
        if ti == 0:
            G(lambda e: e.memset(xcr[:, 2 * g:2 * g + 2, 0:3], 0.0), w=[f"xcr.{cc}" for cc in CS])
            G(lambda e: e.memset(hst[:, 2 * g:2 * g + 2], 0.0), w=[hn])
        for cc in CS:
            pa, pn = proj_fm(wv, cc * 128, pool)
            act(xcr[:, cc, 3:3 + TT], pa, AF.Copy, [], [pn, f"xcr.{cc}"])
        for cc in CS:
            V(lambda e, cc=cc: e.tensor_scalar(out=xc[:, cc, :], in0=xcr[:, cc, 3:3 + TT], scalar1=PP(f"cw{o_}_3", cc), scalar2=PP(f"cb{o_}", cc),
                                               op0=ALU.mult, op1=ALU.add), r=[f"xcr.{cc}", "ppt"], w=[f"xc.{cc}"])
        for j in (2, 1, 0):
            for cc in CS:
                V(lambda e, cc=cc, j=j: e.scalar_tensor_tensor(out=xc[:, cc, :], in0=xcr[:, cc, j:j + TT], scalar=PP(f"cw{o_}_{j}", cc),
                                                               in1=xc[:, cc, :], op0=ALU.mult, op1=ALU.add),
                  r=[f"xcr.{cc}", "ppt"], w=[f"xc.{cc}"])
        for cc in CS:
            V(lambda e, cc=cc: e.tensor_copy(out=xcr[:, cc, 0:3], in_=xcr[:, cc, TT:TT + 3]), w=[f"xcr.{cc}"])
            G(lambda e, cc=cc: e.tensor_copy(out=xcb[:, cc, :], in_=xc[:, cc, :]), r=[f"xc.{cc}"], w=[f"xcb.{cc}"])
        for cc in CS:
            pa, pn = proj_fm(wv, 1024 + cc * 128, pool)
            act(sgc[:, cc, :], pa, AF.Exp, [], [pn, f"sgc.{cc}"], scale=-1.0)
            act(at[:, cc, :], pa, AF.Copy, [], [pn, f"at.{cc}"])
        for cc in CS:
            act(sgc[:, cc, :], sgc[:, cc, :], AF.Ln, ["epsc"], [f"sgc.{cc}"], bias=ONE)
        for cc in CS:
            act(sgc[:, cc, :], sgc[:, cc, :], AF.Exp, [], [f"sgc.{cc}"], scale=-1.0)
        for cc in CS:
            V(lambda e, cc=cc: e.tensor_tensor(out=sgc[:, cc, :], in0=sgc[:, cc, :], in1=at[:, cc, :], op=ALU.mult), r=[f"at.{cc}"], w=[f"sgc.{cc}"])
        for oc in CS:
            jh = oc % 2
            for wi, dst, bn in ((0, gr, f"ba{o_}"), (1, gi, f"bx{o_}")):
                pa, pn = ps(pool)
                for kc in range(2):
                    mm(pa[:, 0:TT], wg[:, wi, g, kc, jh * 128:(jh + 1) * 128], xcb[:, g * 2 + kc, :], ["wgate", f"xcb.{g * 2 + kc}"], [pn],
                       start=(kc == 0), stop=(kc == 1))
                act(dst[:, oc, :], pa[:, 0:TT], AF.Exp, ["negb"], [pn, f"{'gr' if wi == 0 else 'gi'}.{oc}"], scale=-1.0, bias=c.NB_(bn, oc))
        for nm, dst in (("gr", gr), ("gi", gi)):
            for oc in CS:
                act(dst[:, oc, :], dst[:, oc, :], AF.Ln, ["epsc"], [f"{nm}.{oc}"], bias=ONE)
            for oc in CS:
                act(dst[:, oc, :], dst[:, oc, :], AF.Exp, [], [f"{nm}.{oc}"], scale=-1.0)
        for oc in CS:
            act(at[:, oc, :], gr[:, oc, :], AF.Exp, [f"gr.{oc}", "c8t"], [f"at.{oc}"], scale=c.c8t[:, o_, oc:oc + 1])
        for oc in CS:
            act(gr[:, oc, :], at[:, oc, :], AF.Square, [f"at.{oc}"], [f"gr.{oc}"])
        for oc in CS:
            V(lambda e, oc=oc: e.tensor_scalar(out=gr[:, oc, :], in0=gr[:, oc, :], scalar1=-1.0, scalar2=1.0, op0=ALU.mult, op1=ALU.add), w=[f"gr.{oc}"])
        for oc in CS:
            act(gr[:, oc, :], gr[:, oc, :], AF.Ln, ["epsc"], [f"gr.{oc}"], bias=TINY)
        for oc in CS:
            act(gr[:, oc, :], gr[:, oc, :], AF.Exp, [], [f"gr.{oc}"], scale=0.5)
        for oc in CS:
            V(lambda e, oc=oc: e.tensor_tensor(out=gi[:, oc, :], in0=gi[:, oc, :], in1=gr[:, oc, :], op=ALU.mult), r=[f"gr.{oc}"], w=[f"gi.{oc}"])
        for oc in CS:
            V(lambda e, oc=oc: e.tensor_tensor(out=gi[:, oc, :], in0=gi[:, oc, :], in1=xc[:, oc, :], op=ALU.mult), r=[f"xc.{oc}"], w=[f"gi.{oc}"])
        for oc in CS:
            V(lambda e, oc=oc: e.tensor_tensor_scan(out=gr[:, oc, :], data0=at[:, oc, :], data1=gi[:, oc, :], initial=hst[:, oc:oc + 1],
                                                    op0=ALU.mult, op1=ALU.add), r=[f"at.{oc}", f"gi.{oc}", hn], w=[f"gr.{oc}"])
        V(lambda e: e.tensor_copy(out=hst[:, 2 * g:2 * g + 2], in_=gr[:, 2 * g:2 * g + 2, TT - 1]), r=[f"gr.{oc}" for oc in CS], w=[hn])
        for oc in CS:
            V(lambda e, oc=oc: e.tensor_tensor(out=mixed[:, oc, :], in0=gr[:, oc, :], in1=sgc[:, oc, :], op=ALU.mult), r=[f"gr.{oc}", f"sgc.{oc}"], w=[f"{mxn}.{oc}"])

    have_odd = any(l % 2 == 1 for l in layers)
    have_even = any(l % 2 == 0 for l in layers)
    if have_odd:
        alloc_odd()
    if have_even:
        ev = alloc_even(c)
    tile_ctr = 0
    for li, l in enumerate(layers):
        fsc = c.fsc
        S.fence({"vector": lambda e: e.memset(fsc[:, 0:1], 0.0), "scalar": lambda e: e.activation(out=fsc[:, 1:2], in_=c.epsc[:, 0:1], func=AF.Copy),
                 "gpsimd": lambda e: e.memset(fsc[:, 2:3], 0.0)})
        load_weights(l)
        if l % 2 == 0:
            even_layer_setup(c, ev, l)
        seq = [(b, ti) for b in range(NB) for ti in range(NTILE)]
        load_x(li, seq[0][0], seq[0][1], tile_ctr % 2)
        if len(seq) > 1:
            load_x(li, seq[1][0], seq[1][1], (tile_ctr + 1) % 2)
        prevE = []
        for k, (b, ti) in enumerate(seq):
            slot = tile_ctr % 2
            xt = xts[slot]
            P = S.record(lambda: tile_prologue(l, b, xt, slot))
            mixed, mxn = c.mixeds[tile_ctr % 2], f"mixed{tile_ctr % 2}"
            if l % 2 == 1:
                M = merge_threads([S.record(lambda g=g: odd_tile(l, b, ti, xt, slot, mixed, mxn, g)) for g in range(4)])
            else:
                Mh = S.record(lambda: hgrn2_tile(c, ev, l, b, ti, mixed, mxn))
                Mr = S.record(lambda: rwkv_tile(c, ev, l, b, ti, mixed, mxn))
                M = merge_threads([Mh, Mr], prio=[0.0, 0.8])
            E = S.record(lambda: tile_epilogue(li, l, b, ti, xt, slot, mixed, mxn))
            S.replay(merge_threads([prevE, P + M], prio=[0.0, 0.8]))
            if k >= 1 and k + 1 < len(seq):
                load_x(li, seq[k + 1][0], seq[k + 1][1], (tile_ctr + 1) % 2)
            prevE = E
            tile_ctr += 1
        S.replay(prevE)
def make_in_maps(inp, x_full):
    pp = pack_pp(inp)
    maps = []
    shared = {name: np.ascontiguousarray(np.asarray(inp[name], np.float32)) for name, _ in WEIGHT_SPECS}
    for i in range(NCORE):
        xs = np.asarray(x_full[NB * i:NB * (i + 1)], np.float32).reshape(NTOK, D)
        xTc = np.ascontiguousarray(xs.T)
        cs = np.asarray(inp["c"][NB * i:NB * (i + 1)], np.float32)
        cT = np.ascontiguousarray(cs.reshape(NB, 8, 128).transpose(2, 1, 0).reshape(128, 16))
        m = {"xT": xTc, "cT": cT, "pp": pp}
        m.update(shared)
        maps.append(m)
    return maps


_CACHE = {}


def run_layers(inp, x_full, layers=(0, 1, 2, 3), debug=False):
    key = (tuple(layers), debug)
    if key not in _CACHE:
        _CACHE[key] = build(layers=layers, debug=debug)
    nc, S = _CACHE[key]
    maps = make_in_maps(inp, x_full)
    res = run_bass_kernel_spmd(nc, maps, core_ids=list(range(NCORE)))
    outs = []
    for i in range(NCORE):
        o = np.asarray(res.results[i]["outT"])
        outs.append(o.T.reshape(NB, T, D))
    if debug:
        res.dbg = np.asarray(res.results[0]["dbgT"]).astype(np.float32).T.reshape(NB, T, D)
    return np.concatenate(outs, axis=0), res


def kernel(**inputs):
    out, _ = run_layers(inputs, inputs["x"])
    return out.astype(np.float32)


def alloc_even(c):
    ev = types_ns()
    c.roff[0] = 0
    cv = c.carve
    ev.hg = [cv([128, TT]) for _ in range(7)]
    ev.rg = [cv([128, TT + 1]) for _ in range(6)]
    ev.itok = cv([128, NTB, 512], BF16)
    ev.Qb = cv([128, TT], BF16)
    ev.Kend = cv([128, TT], BF16)
    ev.Qmid = cv([128, TT], BF16)
    ev.Kmid = cv([128, TT], BF16)
    ev.KendT = cv([128, NTB, 128], BF16)
    ev.PT = cv([128, NTB, 128], BF16)
    ev.sm = cv([128, 3, NCH])
    ev.Sf = cv([128, 4, 128])
    ev.Sb = cv([128, 4, 2, 128], BF16)
    ev.zc = cv([128, 4])
    ev.zw = cv([128, 4, TT], BF16)
    ev.za = cv([128, 4, TT], BF16)
    ev.zv = cv([128, 4, TT], BF16)
    ev.hid = cv([64, 3, TT], BF16)
    ev.hidf = cv([64, TT])
    ev.lwt = cv([128, TT])
    ev.icl = cv([128, TT])
    ev.vg = cv([128, TT])
    ev.cr = cv([128, 3, 4])
    ev.raw = cv([128, TT + 1])
    ev.kh = cv([128, TT])
    ev.cw = cv([128, TT])
    ev.EW = cv([128, TT])
    ev.rkp = cv([128, TT])
    ev.ARz = cv([128, 2, NCH, 2, 64], BF16)
    ev.Bt = cv([128, TT], BF16)
    ev.Kt = cv([128, TT], BF16)
    ev.Ns = cv([64, NCH, 8, 64], BF16)
    ev.NT0 = cv([64, NCH * 2, 64], BF16)
    ev.LA = cv([64, 2, 8, 64], BF16)
    ev.LB = cv([64, 2, 8, 64], BF16)
    ev.Pq = cv([64, 8, 64], BF16)
    ev.Vtb = cv([64, NCH, 128], BF16)
    ev.BKT = cv([64, NCH, 2, 128], BF16)
    ev.Yall = cv([64, NCH, 128])
    ev.Ysq = cv([64, NCH, 128])
    ev.Xsb = [cv([64, 128], BF16) for _ in range(2)]
    ev.Usb = [cv([64, 128], BF16) for _ in range(2)]
    ev.st = cv([64, 4, 8])
    ev.Hs = cv([128, 4, 64])
    ev.Hb = cv([128, 4, 64], BF16)
    ev.sgb = cv([128, TT])
    ev.wprev = cv([128, 4])
    print("even region words", c.roff[0])
    return ev


def even_layer_setup(c, ev, l):
    e_ = l // 2
    S, V, G = c.S, c.V, c.G
    lora = c.lora
    specs = [("b_w1", e_, 0, 64), ("b_a1", e_, 256, 64)]
    if e_ == 1:
        specs.append(("b_v1", 0, 512, 32))
    for nm, idx, off, r_ in specs:
        st, stn = c.stage_load(c.W[nm][idx].rearrange("(j p) n -> p j n", p=128), 128, 4 * r_,
                               view=lambda s, r_=r_: s[:, 0:4 * r_].rearrange("p (j n) -> p j n", j=4))
        c.cast("vector", lora[:, off:off + 4 * r_], st[:, 0:4 * r_], [stn], ["lora"])
    specs2 = [("b_w2", e_, 640, 64), ("b_a2", e_, 1152, 64)]
    if e_ == 1:
        specs2.append(("b_v2", 0, 1664, 32))
    for nm, idx, off, r_ in specs2:
        st, stn = c.stage_load(c.W[nm][idx], r_, 512)
        c.cast("vector", lora[0:r_, off:off + 512], st[0:r_, 0:512], [stn], ["lora"])
    G(lambda e: e.memset(ev.ARz[:], 0.0), w=["ARz"])


def hgrn2_tile(c, ev, l, b, ti, mixed, mxn):
    S, V, A, G, PE, act, mm, ps, PP, sigm = c.S, c.V, c.A, c.G, c.PE, c.act, c.mm, c.ps, c.PP, c.sigm
    hb, wbig, proj_fm = c.hb, c.wbig, c.proj_fm
    e_ = l // 2
    wv = wbig[:].rearrange("p (dc n) -> p dc n", dc=8)
    C3 = lambda ap: ap.rearrange("p (c t) -> p c t", t=64)
    ONE = c.epsc[:, 3:4]

    def tr(out, in_, ident, r, w):
        return PE(lambda e: e.transpose(out=out, in_=in_, identity=ident), r, w)

    itok, Qb, Kend, Qmid, Kmid, KendT, PT, sm, Sf, Sb = ev.itok, ev.Qb, ev.Kend, ev.Qmid, ev.Kmid, ev.KendT, ev.PT, ev.sm, ev.Sf, ev.Sb
    for tb in range(NTB):
        pa, pn = ps("hps")
        for dc in range(8):
            mm(pa[:, 0:512], hb[:, dc, tb * 128:(tb + 1) * 128], wv[:, dc, 1024:1536], [f"hb.{dc}", f"wbig.{dc}"], [pn],
               start=(dc == 0), stop=(dc == 7))
        act(itok[:, tb, :], pa[:, 0:512], AF.Copy, [], [pn, f"itok.{tb}"])
    if ti == 0:
        G(lambda e: e.memset(Sf[:], 0.0), w=["Sf"])
        G(lambda e: e.memset(Sb[:], 0.0), w=["Sb"])
    q, sg, gs, sgl, kk, bc, ee = ev.hg
    qn, sgn, gsn, sgln, kkn_, bcn, een = [f"hg{i}" for i in range(7)]
    for hd in range(4):
        pq, pqn = proj_fm(wv, hd * 128, "hps")
        act(q[:], pq, AF.Copy, [], [pqn, qn])
        pf, pfn = proj_fm(wv, 512 + hd * 128, "hps")
        sigm(sg[:], sgn, pf, [], w_extra=[pfn])
        pg, pgn = proj_fm(wv, 1536 + hd * 128, "hps")
        act(gs[:], pg, AF.Copy, [], [pgn, gsn])
        sigm(sgl[:], sgln, pg, [], w_extra=[pgn])
        V(lambda e, hd=hd: e.tensor_scalar(out=kk[:], in0=sg[:], scalar1=c.nomlt[:, e_, hd:hd + 1], scalar2=c.omlt[:, e_, hd:hd + 1],
                                           op0=ALU.mult, op1=ALU.add), r=[sgn, "nomlt", "omlt"], w=[kkn_])
        V(lambda e, hd=hd: e.tensor_scalar(out=sg[:], in0=sg[:], scalar1=c.omlt[:, e_, hd:hd + 1], scalar2=c.lbt[:, e_, hd:hd + 1],
                                           op0=ALU.mult, op1=ALU.add), r=["omlt", "lbt"], w=[sgn])
        act(sg[:], sg[:], AF.Ln, [], [sgn])
        V(lambda e: e.tensor_tensor_scan(out=bc[:], data0=c.rm[:], data1=sg[:], initial=0.0, op0=ALU.mult, op1=ALU.add), r=["rm", sgn], w=[bcn])
        bc3 = C3(bc[:])
        act(ee[:], bc[:], AF.Exp, [bcn], [een])
        V(lambda e: e.tensor_tensor(out=Qb[:], in0=q[:], in1=ee[:], op=ALU.mult), r=[qn, een], w=["Qb"])
        V(lambda e: e.tensor_tensor(out=C3(sg[:]), in0=bc3[:, :, 63:64].to_broadcast([128, NCH, 64]), in1=bc3, op=ALU.subtract), r=[bcn], w=[sgn])
        act(sg[:], sg[:], AF.Exp, [], [sgn])
        V(lambda e: e.tensor_tensor(out=Kend[:], in0=kk[:], in1=sg[:], op=ALU.mult), r=[kkn_, sgn], w=["Kend"])
        act(sm[:, 0, :], bc3[:, :, 31], AF.Exp, [bcn], ["sm"], scale=-1.0)
        V(lambda e: e.tensor_tensor(out=sm[:, 1, :], in0=bc3[:, :, 31], in1=bc3[:, :, 63], op=ALU.subtract), r=[bcn], w=["sm"])
        act(sm[:, 1, :], sm[:, 1, :], AF.Exp, [], ["sm"])
        act(sm[:, 2, :], bc3[:, :, 63], AF.Exp, [bcn], ["sm"])
        V(lambda e: e.tensor_tensor(out=C3(Qmid[:]), in0=C3(Qb[:]), in1=sm[:, 0, :].unsqueeze(2).to_broadcast([128, NCH, 64]), op=ALU.mult),
          r=["Qb", "sm"], w=["Qmid"])
        V(lambda e: e.tensor_tensor(out=C3(Kmid[:]), in0=C3(Kend[:]), in1=sm[:, 1, :].unsqueeze(2).to_broadcast([128, NCH, 64]), op=ALU.mult),
          r=["Kend", "sm"], w=["Kmid"])
        V(lambda e: e.tensor_tensor(out=gs[:], in0=gs[:], in1=sgl[:], op=ALU.mult), r=[sgln], w=[gsn])
        for tb in range(NTB):
            ts_ = slice(tb * 128, (tb + 1) * 128)
            ptr, ptrn = ps("hps")
            ptb = ptr[:, 0:64].bitcast(BF16)
            tr(ptb, Kend[:, ts_], c.identb[:], ["Kend", "identb"], [ptrn])
            act(KendT[:, tb, :], ptb, AF.Copy, [], [ptrn, "KendT"])
            psc, pscn = ps("hps")
            mm(psc[:, 0:128], Kmid[:, ts_], Qmid[:, ts_], ["Kmid", "Qmid"], [pscn])
            V(lambda e, tb=tb, psc=psc: e.tensor_tensor(out=PT[:, tb, :], in0=psc[:, 0:128], in1=c.mHf[:], op=ALU.mult), r=["mHf"], w=[pscn, "PT"])
        po, pon = ps("hacc")
        for tb in range(NTB):
            for cl in range(2):
                cg = tb * 2 + cl
                par = (ti * NCH + cg) % 2
                mm(po[:, cg * 64:(cg + 1) * 64], itok[:, tb, hd * 128:(hd + 1) * 128], PT[:, tb, cl * 64:(cl + 1) * 64], [f"itok.{tb}", "PT"], [pon],
                   start=True, stop=False)
                mm(po[:, cg * 64:(cg + 1) * 64], Sb[:, hd, par, :], Qb[:, cg * 64:(cg + 1) * 64], ["Sb", "Qb"], [pon], start=False, stop=True)
                pd, pdn = ps("hps")
                rs = slice(cl * 64, (cl + 1) * 64)
                mm(pd[:, 0:128], KendT[rs, tb, :], itok[rs, tb, hd * 128:(hd + 1) * 128], ["KendT", f"itok.{tb}"], [pdn])
                V(lambda e, hd=hd, cg=cg, pd=pd: e.scalar_tensor_tensor(out=Sf[:, hd, :], in0=Sf[:, hd, :], scalar=sm[:, 2, cg:cg + 1], in1=pd[:, 0:128],
                                                                        op0=ALU.mult, op1=ALU.add), r=["sm"], w=[pdn, "Sf"])
                act(Sb[:, hd, 1 - par, :], Sf[:, hd, :], AF.Copy, ["Sf"], ["Sb"])
        eeb = ee.bitcast(BF16)[:, 0:TT]
        act(eeb, po[:, 0:TT], AF.Square, [], [pon, een])
        pss, pssn = ps("hps")
        mm(pss[:, 0:TT], c.onesb[:], eeb, ["onesb", een], [pssn])
        act(q[:], pss[:, 0:TT], AF.Ln, ["epsc"], [pssn, qn], scale=1.0 / 128.0, bias=c.epsc[:, 1:2])
        act(q[:], q[:], AF.Exp, [], [qn], scale=-0.5)
        V(lambda e, po=po: e.tensor_tensor(out=kk[:], in0=po[:, 0:TT], in1=q[:], op=ALU.mult), r=[qn], w=[pon, kkn_])
        V(lambda e, hd=hd: e.scalar_tensor_tensor(out=mixed[:, hd, :], in0=kk[:], scalar=PP(f"ang{e_}", hd), in1=gs[:], op0=ALU.mult, op1=ALU.mult),
          r=[kkn_, gsn, "ppt"], w=[f"{mxn}.{hd}"])


def rwkv_tile(c, ev, l, b, ti, mixed, mxn):
    S, V, A, G, PE, act, mm, ps, PP, sigm = c.S, c.V, c.A, c.G, c.PE, c.act, c.mm, c.ps, c.PP, c.sigm
    hb, wbig, proj_fm = c.hb, c.wbig, c.proj_fm
    e_ = l // 2
    tok0 = b * T + ti * TT
    wv = wbig[:].rearrange("p (dc n) -> p dc n", dc=8)
    C3 = lambda ap: ap.rearrange("p (c t) -> p c t", t=64)
    ONE, MONE = c.epsc[:, 3:4], c.epsc[:, 4:5]
    DK = float(np.exp(-0.5))

    def tr(out, in_, ident, r, w):
        return PE(lambda e: e.transpose(out=out, in_=in_, identity=ident), r, w)

    zc, zw, za, zv, hid, hidf, lwt, icl, vg, cr, raw = ev.zc, ev.zw, ev.za, ev.zv, ev.hid, ev.hidf, ev.lwt, ev.icl, ev.vg, ev.cr, ev.raw
    kh, cw, EW, rkp, ARz, Bt, Kt, Ns, NT0, LA, LB, Pq = ev.kh, ev.cw, ev.EW, ev.rkp, ev.ARz, ev.Bt, ev.Kt, ev.Ns, ev.NT0, ev.LA, ev.LB, ev.Pq
    Vtb, BKT, Yall, Ysq, st, Hs, Hb, sgb = ev.Vtb, ev.BKT, ev.Yall, ev.Ysq, ev.st, ev.Hs, ev.Hb, ev.sgb
    lora, vxo, vft = c.lora, c.vxo, c.vft
    lw1 = lora[:, 0:256].rearrange("p (j n) -> p j n", j=4)
    la1 = lora[:, 256:512].rearrange("p (j n) -> p j n", j=4)
    lv1 = lora[:, 512:640].rearrange("p (j n) -> p j n", j=4)
    lw2, la2, lv2 = lora[0:64, 640:1152], lora[0:64, 1152:1664], lora[0:32, 1664:2176]
    R0, K0, V0, Z0, G0 = 2048, 2560, 3072, 3584, 4096
    rg = ev.rg
    rgn = [f"rg{i}" for i in range(6)]
    if ti == 0:
        G(lambda e: e.memset(zc[:], 0.0), w=["zc"])
        G(lambda e: e.memset(cr[:], 0.0), w=["cr"])
        G(lambda e: e.memset(Hs[:], 0.0), w=[f"Hs.{j}" for j in range(4)])
        G(lambda e: e.memset(Hb[:], 0.0), w=[f"Hb.{j}" for j in range(4)])
    dz, dzn = rg[4][:, 0:TT], rgn[4]
    for j in range(4):
        zr, zrn = rg[j], rgn[j]
        G(lambda e, j=j, zr=zr: e.tensor_copy(out=zr[:, 0:1], in_=zc[:, j:j + 1]), r=["zc"], w=[zrn])
        pz, pzn = proj_fm(wv, Z0 + j * 128, "rproj")
        act(zr[:, 1:1 + TT], pz, AF.Copy, [], [pzn, zrn])
        G(lambda e, j=j, zr=zr: e.tensor_copy(out=zc[:, j:j + 1], in_=zr[:, TT:TT + 1]), r=[zrn], w=["zc"])
        V(lambda e, zr=zr: e.tensor_tensor(out=dz, in0=zr[:, 0:TT], in1=zr[:, 1:1 + TT], op=ALU.subtract), r=[zrn], w=[dzn])
        lst = [(zw, f"mu{e_}_3"), (za, f"mu{e_}_4")] + ([(zv, "vmu")] if e_ == 1 else [])
        for dst, mun in lst:
            V(lambda e, j=j, dst=dst, mun=mun, zr=zr: e.scalar_tensor_tensor(out=dst[:, j, :], in0=dz, scalar=PP(mun, j), in1=zr[:, 1:1 + TT],
                                                                           op0=ALU.mult, op1=ALU.add), r=[dzn, zrn, "ppt"], w=["zlerp"])
    ph, phn = ps("rmm")
    for j in range(4):
        mm(ph[0:64, 0:TT], lw1[:, j, :], zw[:, j, :], ["lora", "zlerp"], [phn], start=(j == 0), stop=(j == 3))
    sigm(hidf[:], "hidf", ph[0:64, 0:TT], [], w_extra=[phn], xscale=2.0)
    act(hid[:, 0, :], hidf[:], AF.Identity, ["hidf", "epsc"], ["hid"], scale=2.0, bias=MONE[0:64, :])
    ph, phn = ps("rmm")
    for j in range(4):
        mm(ph[0:64, 0:TT], la1[:, j, :], za[:, j, :], ["lora", "zlerp"], [phn], start=(j == 0), stop=(j == 3))
    act(hid[:, 1, :], ph[0:64, 0:TT], AF.Copy, [], [phn, "hid"])
    if e_ == 1:
        ph, phn = ps("rmm")
        for j in range(4):
            mm(ph[0:32, 0:TT], lv1[:, j, :], zv[:, j, :], ["lora", "zlerp"], [phn], start=(j == 0), stop=(j == 3))
        act(hid[0:32, 2, :], ph[0:32, 0:TT], AF.Copy, [], [phn, "hid"])

    rr, kx, t1, t2, t3, kkn = [g_[:, 0:TT] for g_ in rg]
    rrn, kxn, t1n, t2n, t3n, kknn = rgn
    wprev = ev.wprev
    if ti == 0:
        G(lambda e: e.memset(wprev[:], 1.0), w=["wprev"])

    def AU(f):
        return [S.record(f)]

    def OU(f):
        return S.record(f)

    def do_j(j):
        js = slice(j * 128, (j + 1) * 128)

        def lora_sig(dst, dn, w2, hrow, hid_ap, nb):
            def u():
                p_, pn_ = ps("rmm")
                mm(p_[:, 0:TT], w2[:, js], hid_ap, ["lora", "hid"], [pn_])
                act(dst[:], p_[:, 0:TT], AF.Exp, ["negb"], [pn_, dn], scale=-1.0, bias=nb)
            def rest():
                act(dst[:], dst[:], AF.Ln, ["epsc"], [dn], bias=ONE)
                act(dst[:], dst[:], AF.Exp, [], [dn], scale=-1.0)
            return AU(u) + OU(rest)

        T_lwt = lora_sig(lwt, "lwt", lw2, 64, hid[:, 0, :], c.NB_(f"w0{e_}", j))
        T_icl = lora_sig(icl, "icl", la2, 64, hid[:, 1, :], c.NB_(f"a0{e_}", j))
        T_vg = lora_sig(vg, "vg", lv2, 32, hid[0:32, 2, :], c.NB_("v0", j)) if e_ == 1 else []

        def g_unit():
            pg_, pgn_ = proj_fm(wv, G0 + j * 128, "rproj")
            act(t3, pg_, AF.Copy, [], [pgn_, t3n])
            act(sgb[:], pg_, AF.Exp, [], [pgn_, "sgb"], scale=-1.0)

        def g_rest():
            act(sgb[:], sgb[:], AF.Ln, ["epsc"], ["sgb"], bias=ONE)
            act(sgb[:], sgb[:], AF.Exp, [], ["sgb"], scale=-1.0)
            V(lambda e: e.tensor_tensor(out=sgb[:], in0=sgb[:], in1=t3, op=ALU.mult), r=[t3n], w=["sgb"])
        T_g = AU(g_unit) + OU(g_rest)

        T_rkv = []
        for kind, col0, dst, dn in ((0, R0, rr, rrn), (1, K0, kx, kxn), (2, V0, vxo[:], "vxo")):
            def pre(kind=kind):
                G(lambda e: e.tensor_copy(out=raw[:, 0:1], in_=cr[:, kind, j:j + 1]), r=["cr"], w=["raw"])
            def pu(col0=col0):
                pp_, ppn = proj_fm(wv, col0 + j * 128, "rproj")
                act(raw[:, 1:1 + TT], pp_, AF.Copy, [], [ppn, "raw"])
            def post(kind=kind, dst=dst, dn=dn):
                G(lambda e: e.tensor_copy(out=cr[:, kind, j:j + 1], in_=raw[:, TT:TT + 1]), r=["raw"], w=["cr"])
                V(lambda e: e.tensor_tensor(out=t1, in0=raw[:, 0:TT], in1=raw[:, 1:1 + TT], op=ALU.subtract), r=["raw"], w=[t1n])
                V(lambda e: e.scalar_tensor_tensor(out=dst, in0=t1, scalar=PP(f"mu{e_}_{kind}", j), in1=raw[:, 1:1 + TT],
                                                   op0=ALU.mult, op1=ALU.add), r=[t1n, "raw", "ppt"], w=[dn])
            T_rkv += OU(pre) + AU(pu) + OU(post)

        def vres():
            if e_ == 0:
                S.dma(c.vfd[js, tok0:tok0 + TT], vxo[:], "vxst", reads=["vxo"], writes=[f"vfd.{b}.{ti}.{j}"])
            else:
                S.dma(vft[:], c.vfd[js, tok0:tok0 + TT], "vft", reads=[f"vfd.{b}.{ti}.{j}"], writes=["vft"])
                V(lambda e: e.tensor_tensor(out=t1, in0=vft[:], in1=vxo[:], op=ALU.subtract), r=["vft", "vxo"], w=[t1n])
                V(lambda e: e.tensor_tensor(out=t1, in0=t1, in1=vg[:], op=ALU.mult), r=["vg"], w=[t1n])
                V(lambda e: e.tensor_tensor(out=vxo[:], in0=vxo[:], in1=t1, op=ALU.add), r=[t1n], w=["vxo"])
        S.replay(merge_threads([T_lwt, T_icl, T_vg, T_g, T_rkv]))
        vres()

        def kk1():
            V(lambda e: e.tensor_scalar_mul(out=t1, in0=kx, scalar1=PP(f"kk{e_}", j)), r=[kxn, "ppt"], w=[t1n])
            act(t2.bitcast(BF16)[:, 0:TT], t1, AF.Square, [t1n], [t2n])
        def kk2():
            pk, pkn = ps("rmm")
            mm(pk[:, 0:TT], c.bob[:], t2.bitcast(BF16)[:, 0:TT], ["bob", t2n], [pkn])
            V(lambda e, pk=pk: e.tensor_scalar_max(out=t2, in0=pk[:, 0:TT], scalar1=1e-24), w=[pkn, t2n])
        def kk3():
            act(t2, t2, AF.Ln, [], [t2n])
            act(t2, t2, AF.Exp, [], [t2n], scale=-0.5)
            V(lambda e: e.tensor_tensor(out=kkn, in0=t1, in1=t2, op=ALU.mult), r=[t1n, t2n], w=[kknn])
            V(lambda e: e.tensor_tensor(out=t2, in0=kkn, in1=icl[:], op=ALU.mult), r=[kknn, "icl"], w=[t2n])
        T_kk = OU(kk1) + AU(kk2) + OU(kk3)

        def khf():
            V(lambda e: e.tensor_scalar(out=t3, in0=icl[:], scalar1=PP(f"ka{e_}", j), scalar2=c.omka[:, e_, j:j + 1], op0=ALU.mult, op1=ALU.add),
              r=["icl", "ppt", "omka"], w=[t3n])
            V(lambda e: e.tensor_tensor(out=kh[:], in0=kx, in1=t3, op=ALU.mult), r=[kxn, t3n], w=["kh"])
            V(lambda e: e.scalar_tensor_tensor(out=rkp.bitcast(BF16)[:, 0:TT], in0=rr, scalar=PP(f"rk{e_}", j), in1=kh[:], op0=ALU.mult, op1=ALU.mult),
              r=[rrn, "kh", "ppt"], w=["rkp"])
        T_kh = OU(khf)

        def cwf():
            V(lambda e: e.tensor_tensor_scan(out=cw[:], data0=c.rm[:], data1=lwt[:], initial=0.0, op0=ALU.mult, op1=ALU.add), r=["rm", "lwt"], w=["cw"])
            act(EW[:], cw[:], AF.Exp, ["cw"], ["EW"], scale=-DK)
            V(lambda e: e.tensor_tensor(out=lwt[:], in0=cw[:], in1=lwt[:], op=ALU.subtract), r=["cw"], w=["lwt"])
            act(lwt[:], lwt[:], AF.Exp, [], ["lwt"], scale=-DK)
            act(cw[:], cw[:], AF.Exp, [], ["cw"], scale=DK)
        T_cw = OU(cwf)
        S.replay(merge_threads([T_kk, T_kh, T_cw]))
        for h in range(2):
            rs = slice(h * 64, (h + 1) * 64)
            V(lambda e, h=h, rs=rs: e.scalar_tensor_tensor(out=ARz[rs, h, :, 0, :], in0=C3(kkn[rs, :]), scalar=-1.0, in1=C3(lwt[rs, :]),
                                                          op0=ALU.mult, op1=ALU.mult), r=[kknn, "lwt"], w=["ARz"])
            V(lambda e, h=h, rs=rs: e.tensor_tensor(out=ARz[rs, h, :, 1, :], in0=C3(rr[rs, :]), in1=C3(EW[rs, :]), op=ALU.mult), r=[rrn, "EW"], w=["ARz"])
        V(lambda e: e.tensor_tensor(out=Bt[:], in0=t2, in1=cw[:], op=ALU.mult), r=[t2n, "cw"], w=["Bt"])
        V(lambda e: e.tensor_tensor(out=Kt[:], in0=kh[:], in1=cw[:], op=ALU.mult), r=["kh", "cw"], w=["Kt"])

        def chunk_thread(cg):
            cs = slice(cg * 64, (cg + 1) * 64)
            def u1():
                pv, pvn = ps("rmm")
                tr(pv[0:64, 0:128], vxo[:, cs], c.identf[:], ["vxo", "identf"], [pvn])
                act(Vtb[:, cg, :], pv[0:64, 0:128], AF.Copy, [], [pvn, f"Vtb.{cg}"])
            def u2():
                ptr, ptrn = ps("rmm")
                ptb = ptr[0:64, 0:128].bitcast(BF16)
                tr(ptb[:, 0:128], Bt[:, cs], c.identb[:], ["Bt", "identb"], [ptrn])
                tr(ptb[:, 128:256], Kt[:, cs], c.identb[:], ["Kt", "identb"], [ptrn])
                act(BKT[:, cg, :, :], ptb.rearrange("p (a n) -> p a n", a=2), AF.Copy, [], [ptrn, f"BKT.{cg}"])
            def u3():
                psn, psnn = ps("rmm")
                for h in range(2):
                    arz = ARz[:, h, cg, :, :].rearrange("p a n -> p (a n)")
                    mm(psn[0:64, h * 256:h * 256 + 128], Bt[:, cs], arz, ["Bt", "ARz"], [psnn])
                    mm(psn[0:64, h * 256 + 128:h * 256 + 256], Kt[:, cs], arz, ["Kt", "ARz"], [psnn])
                V(lambda e, psn=psn: e.tensor_tensor(out=Ns[:, cg, :, :].rearrange("p a n -> p (a n)"), in0=psn[0:64, :],
                                                     in1=c.mask512[:].rearrange("p a n -> p (a n)"), op=ALU.mult), r=["mask512"], w=[psnn, f"Ns.{cg}"])
            def u4():
                pt_, ptn = ps("rmm")
                for h in range(2):
                    mm(pt_[0:64, h * 64:(h + 1) * 64], ARz[:, h, cg, 0, :], Bt[:, cs], ["ARz", "Bt"], [ptn])
                V(lambda e, pt_=pt_: e.tensor_tensor(out=NT0[:, cg * 2:(cg + 1) * 2, :].rearrange("p a n -> p (a n)"), in0=pt_[0:64, 0:128],
                                                     in1=c.mL2[:].rearrange("p a n -> p (a n)"), op=ALU.mult), r=["mL2"], w=[ptn, f"NT0.{cg}"])
            return AU(u3) + AU(u4) + AU(u1) + AU(u2)
        S.replay(merge_threads([chunk_thread(cg) for cg in range(NCH)]))
        NsA = [f"Ns.{cg}" for cg in range(NCH)]
        NT0A = [f"NT0.{cg}" for cg in range(NCH)]
        Ns5 = Ns[:].rearrange("p c (h k) n -> p c h k n", h=2)
        N0v = Ns5[:, :, :, 0, :]
        V(lambda e: e.tensor_tensor(out=Pq[:].rearrange("p (c h) n -> p c h n", h=2), in0=N0v,
                                    in1=c.id8[:].rearrange("p (c h) n -> p c h n", h=2), op=ALU.add), r=NsA + ["id8"], w=["Pq"])

        def Nprev(lev, q_):
            if lev == 1:
                return N0v[:, q_ // 2, q_ % 2, :], f"Ns.{q_ // 2}"
            arr = LA if (lev - 1) % 2 == 1 else LB
            return arr[:, 0, q_, :], ("LA" if (lev - 1) % 2 == 1 else "LB")

        def NTprev(lev, q_):
            if lev == 1:
                return NT0[:, q_, :], f"NT0.{q_ // 2}"
            arr = LA if (lev - 1) % 2 == 1 else LB
            return arr[:, 1, q_, :], ("LA" if (lev - 1) % 2 == 1 else "LB")

        for lev in range(1, 6):
            dst = LA if lev % 2 == 1 else LB
            dstn = "LA" if lev % 2 == 1 else "LB"
            need_n = lev < 5
            if need_n:
                pN, pNn = ps("rmm")
                for q_ in range(8):
                    n_ap, n_nm = Nprev(lev, q_)
                    nt_ap, nt_nm = NTprev(lev, q_)
                    mm(pN[0:64, q_ * 64:(q_ + 1) * 64], nt_ap, n_ap, [n_nm, nt_nm], [pNn])
            pNT, pNTn = ps("rmm")
            for q_ in range(8):
                n_ap, n_nm = Nprev(lev, q_)
                nt_ap, nt_nm = NTprev(lev, q_)
                mm(pNT[0:64, q_ * 64:(q_ + 1) * 64], n_ap, nt_ap, [n_nm, nt_nm], [pNTn])
            if need_n:
                act(dst[:, 0, :, :].rearrange("p a n -> p (a n)"), pN[0:64, :], AF.Copy, [], [pNn, dstn])
            V(lambda e, dst=dst, pNT=pNT: e.tensor_copy(out=dst[:, 1, :, :].rearrange("p a n -> p (a n)"), in_=pNT[0:64, :]), w=[pNTn, dstn])
            pP, pPn = ps("rmm")
            for q_ in range(8):
                mm(pP[0:64, q_ * 64:(q_ + 1) * 64], dst[:, 1, q_, :], Pq[:, q_, :], [dstn, "Pq"], [pPn])
            V(lambda e, pP=pP: e.tensor_tensor(out=Pq[:].rearrange("p a n -> p (a n)"), in0=Pq[:].rearrange("p a n -> p (a n)"), in1=pP[0:64, :], op=ALU.add),
              w=[pPn, "Pq"])
        Hbj = Hb[:, j, :]
        for cg in range(NCH):
            pc, pcn = ps("racc")
            Xs, Us = ev.Xsb[cg % 2], ev.Usb[cg % 2]
            xn, un = f"Xsb{cg % 2}", f"Usb{cg % 2}"
            nsn, vtn, bkn = f"Ns.{cg}", f"Vtb.{cg}", f"BKT.{cg}"
            for h in range(2):
                hs = slice(h * 64, (h + 1) * 64)
                mm(pc[0:64, hs], ARz[:, h, cg, 0, :], Hbj, ["ARz", f"Hb.{j}"], [pcn], start=True, stop=False)
                mm(pc[0:64, hs], Ns[:, cg, h * 4 + 2, :], Vtb[:, cg, hs], [nsn, vtn], [pcn], start=False, stop=True)
            act(Xs[:], pc[0:64, 0:128], AF.Copy, [], [pcn, xn])
            for h in range(2):
                hs = slice(h * 64, (h + 1) * 64)
                mm(pc[0:64, 128 + h * 64:128 + (h + 1) * 64], Pq[:, cg * 2 + h, :], Xs[:, hs], ["Pq", xn], [pcn])
            act(Us[:], pc[0:64, 128:256], AF.Copy, [], [pcn, un])
            for h in range(2):
                hs = slice(h * 64, (h + 1) * 64)
                ys = slice(256 + h * 64, 256 + (h + 1) * 64)
                mm(pc[0:64, ys], ARz[:, h, cg, 1, :], Hbj, ["ARz", f"Hb.{j}"], [pcn], start=True, stop=False)
                mm(pc[0:64, ys], Ns[:, cg, h * 4 + 1, :], Us[:, hs], [nsn, un], [pcn], start=False, stop=False)
                mm(pc[0:64, ys], Ns[:, cg, h * 4 + 3, :], Vtb[:, cg, hs], [nsn, vtn], [pcn], start=False, stop=True)
            mm(pc[:, 384:512], BKT[:, cg, 0, :], Us[:], [bkn, un], [pcn], start=True, stop=False)
            mm(pc[:, 384:512], BKT[:, cg, 1, :], Vtb[:, cg, :], [bkn, vtn], [pcn], start=False, stop=True)
            wsc = wprev[:, j:j + 1] if cg == 0 else EW[:, cg * 64 - 1:cg * 64]
            for h in range(2):
                rs = slice(h * 64, (h + 1) * 64)
                V(lambda e, h=h, rs=rs, pc=pc, wsc=wsc: e.scalar_tensor_tensor(out=Hs[rs, j, :], in0=Hs[rs, j, :], scalar=wsc[rs, :],
                                                                              in1=pc[rs, 384 + h * 64:384 + (h + 1) * 64], op0=ALU.mult, op1=ALU.add),
                  r=["EW", "wprev"], w=[pcn, f"Hs.{j}"])
            act(Hb[:, j, :], Hs[:, j, :], AF.Copy, [f"Hs.{j}", "EW"], [f"Hb.{j}"], scale=EW[:, cg * 64 + 63:cg * 64 + 64])
            act(Yall[:, cg, :], pc[0:64, 256:384], AF.Copy, [], [pcn, "Yall"])
        V(lambda e, j=j: e.tensor_copy(out=wprev[:, j:j + 1], in_=EW[:, TT - 1:TT]), r=["EW"], w=["wprev"])
        Y8 = Yall[:].rearrange("p c (h n) -> p (c h) n", h=2)
        Q8 = Ysq[:].rearrange("p c (h n) -> p (c h) n", h=2)
        V(lambda e: e.tensor_reduce(out=st[:, 0, :], in_=Y8, axis=AX.X, op=ALU.add), r=["Yall"], w=["st"])
        act(Ysq[:], Yall[:], AF.Square, ["Yall"], ["Ysq"])
        V(lambda e: e.tensor_reduce(out=st[:, 1, :], in_=Q8, axis=AX.X, op=ALU.add), r=["Ysq"], w=["st"])
        V(lambda e: e.tensor_scalar_mul(out=st[:, 0, :], in0=st[:, 0, :], scalar1=1.0 / 64.0), w=["st"])
        V(lambda e: e.tensor_tensor(out=st[:, 2, :], in0=st[:, 0, :], in1=st[:, 0, :], op=ALU.mult), w=["st"])
        V(lambda e: e.scalar_tensor_tensor(out=st[:, 1, :], in0=st[:, 1, :], scalar=1.0 / 64.0, in1=st[:, 2, :], op0=ALU.mult, op1=ALU.subtract), w=["st"])
        act(st[:, 1, :], st[:, 1, :], AF.Ln, ["epsc"], ["st"], bias=c.epsc[0:64, 2:3])
        act(st[:, 1, :], st[:, 1, :], AF.Exp, [], ["st"], scale=-0.5)
        V(lambda e: e.tensor_tensor(out=Q8, in0=Y8, in1=st[:, 0, :].unsqueeze(2).to_broadcast([64, 8, 64]), op=ALU.subtract), r=["Yall", "st"], w=["Ysq"])
        V(lambda e: e.tensor_tensor(out=Q8, in0=Q8, in1=st[:, 1, :].unsqueeze(2).to_broadcast([64, 8, 64]), op=ALU.mult), r=["st"], w=["Ysq"])
        pf_, pfn_ = ps("rmm")
        for cg in range(NCH):
            tr(pf_[:, cg * 64:(cg + 1) * 64], Ysq[:, cg, :], c.identf[0:64, 0:64], ["Ysq", "identf"], [pfn_])
        act(t1, pf_[:, 0:TT], AF.Identity, ["ppt"], [pfn_, t1n], scale=PP(f"gng{e_}", j), bias=PP(f"gnb{e_}", j))
        pr_, prn_ = ps("rmm")
        mm(pr_[:, 0:TT], c.bob[:], rkp.bitcast(BF16)[:, 0:TT], ["bob", "rkp"], [prn_])
        V(lambda e, pr_=pr_: e.tensor_tensor(out=t2, in0=pr_[:, 0:TT], in1=vxo[:], op=ALU.mult), r=["vxo"], w=[prn_, t2n])
        V(lambda e: e.tensor_tensor(out=t1, in0=t1, in1=t2, op=ALU.add), r=[t2n], w=[t1n])
        V(lambda e, j=j: e.tensor_tensor(out=mixed[:, 4 + j, :], in0=t1, in1=sgb[:], op=ALU.mult), r=[t1n, "sgb"], w=[f"{mxn}.{4 + j}"])

    for j in range(4):
        do_j(j)
```

```python
import numpy as np
from contextlib import ExitStack
import concourse.bass as bass
import concourse.mybir as mybir
from concourse.bass_utils import run_bass_kernel_spmd

F32 = mybir.dt.float32
BF16 = mybir.dt.bfloat16
ALU = mybir.AluOpType
AF = mybir.ActivationFunctionType
AX = mybir.AxisListType

NCORE = 8
D = 1024
T = 2048
NB = 2
NTOK = NB * T
TT = 256
NTILE = T // TT
NCH = TT // 64
NTB = TT // 128
DEPTH = 4
ALPHA = (2.0 * DEPTH) ** 0.25
LN_EPS = 1e-5
RMS_EPS = 1e-6
GN_EPS = 64 * 1e-5
EVC = 4608
STG = 1152
EMBED_WAIT = True


class Buf:
    __slots__ = ("name", "last_w", "readers")

    def __init__(self, name):
        self.name = name
        self.last_w = None
        self.readers = {}


class Op:
    __slots__ = ("eng", "fn", "deps", "signaled", "value", "sem", "is_dma", "grp")

    def __init__(self, eng, fn, is_dma):
        self.eng = eng
        self.fn = fn
        self.deps = []
        self.signaled = False
        self.value = None
        self.sem = None
        self.is_dma = is_dma
        self.grp = None


ENGS = ("tensor", "vector", "scalar", "gpsimd", "sync")


class Sched:
    def __init__(self, nc):
        self.nc = nc
        self.ops = []
        self.bufs = {}
        self.last = {}
        self.rec = None

    def bf(self, name):
        b = self.bufs.get(name)
        if b is None:
            b = self.bufs[name] = Buf(name)
        return b

    def op(self, eng, fn, reads=(), writes=(), dma=False, grp=None, holder=None):
        if self.rec is not None:
            holder = [None]
            self.rec.append((eng, fn, tuple(reads), tuple(writes), dma, grp, holder))
            return holder
        o = Op(eng, fn, dma)
        if dma:
            o.grp = grp
            o.signaled = True
        deps = {}
        reads = [self.bf(r) for r in reads]
        writes = [self.bf(w) for w in writes]
        for r in reads:
            if r.last_w is not None:
                deps[id(r.last_w)] = (r.last_w, True)
        for w in writes:
            if w.last_w is not None:
                deps[id(w.last_w)] = (w.last_w, True)
            for rd in w.readers.values():
                deps.setdefault(id(rd), (rd, False))
        for p, raw in deps.values():
            if p is o:
                continue
            same = (p.eng == eng) and (not p.is_dma)
            if same and not dma and eng == "tensor":
                continue
            o.deps.append(p)
            p.signaled = True
        for r in reads:
            r.readers[("dma", id(o)) if dma else eng] = o
        for w in writes:
            w.last_w = o
            w.readers = {}
        self.ops.append(o)
        if not dma:
            self.last[eng] = o
        if holder is not None:
            holder[0] = o
        return o

    def record(self, f):
        outer = self.rec
        self.rec = []
        f()
        r, self.rec = self.rec, outer
        return r

    def replay(self, lst):
        if self.rec is not None:
            self.rec.extend(lst)
            return
        for eng, fn, reads, writes, dma, grp, holder in lst:
            self.op(eng, fn, reads, writes, dma=dma, grp=grp, holder=holder)

    def fence(self, fns):
        prev = dict(self.last)
        for eng, fn in fns.items():
            o = self.op(eng, fn)
            for pe, p in prev.items():
                if pe != eng and p not in o.deps:
                    o.deps.append(p)
                    p.signaled = True

    def dma(self, out, in_, group, reads=(), writes=(), eng="sync"):
        return self.op(eng, lambda e: e.dma_start(out=out, in_=in_), reads, writes, dma=True, grp=group)

    def emit(self, final_wait_ops=()):
        nc = self.nc
        with ExitStack() as es:
            sems = {e: es.enter_context(nc.semaphore("s_" + e)) for e in ENGS}
            dma_sems = {}
            cnt = {e: 0 for e in ENGS}
            dcnt = {}
            for o in self.ops:
                if o.is_dma:
                    key = o.grp
                    if key not in dma_sems:
                        dma_sems[key] = es.enter_context(nc.semaphore("d_" + str(key)))
                        dcnt[key] = 0
                    dcnt[key] += 16
                    o.sem = dma_sems[key]
                    o.value = dcnt[key]
                else:
                    if o.signaled:
                        cnt[o.eng] += 1
                        o.value = cnt[o.eng]
                    o.sem = sems[o.eng]
            finals = [p[0] if isinstance(p, list) else p for p in final_wait_ops]
            block = es.enter_context(nc.Block())
            per_eng = {e: [o for o in self.ops if o.eng == e] for e in ENGS}

            def make(ename):
                def body(eng):
                    waited = {}
                    for o in per_eng[ename]:
                        need = []
                        for p in o.deps:
                            k = id(p.sem)
                            if waited.get(k, 0) >= p.value:
                                continue
                            waited[k] = p.value
                            need = [q for q in need if q[0] is not p.sem] + [(p.sem, p.value)]
                        emb = None
                        if need and EMBED_WAIT and not o.is_dma:
                            emb = need.pop()
                        for sem_, val_ in need:
                            eng.wait_ge(sem_, val_)
                        ins = o.fn(eng)
                        if emb is not None:
                            ins._wait_ge(emb[0], emb[1])
                        if o.is_dma:
                            ins.then_inc(o.sem, 16)
                        elif o.signaled:
                            ins.then_inc(o.sem, 1)
                    if ename == "sync":
                        for p in finals:
                            eng.wait_ge(p.sem, p.value)
                return body

            for e in ENGS:
                if per_eng[e] or (e == "sync" and finals):
                    getattr(block, e)(make(e))
            self.stats = {e: len(per_eng[e]) for e in ENGS}


def pp_layout():
    lay = {}
    col = [0]

    def add(name, n):
        lay[name] = (col[0], n)
        col[0] += n

    for l in range(DEPTH):
        add(f"adab{l}", 24)
        add(f"lng{l}", 8)
        add(f"lnb{l}", 8)
    for e in range(2):
        add(f"lbl{e}", 4)
        add(f"ang{e}", 4)
        for i in range(5):
            add(f"mu{e}_{i}", 4)
        for nm in ("w0", "a0", "kk", "ka", "rk", "gng", "gnb"):
            add(f"{nm}{e}", 4)
    add("vmu", 4)
    add("v0", 4)
    for o in range(2):
        for j in range(4):
            add(f"cw{o}_{j}", 8)
        for nm in ("cb", "ba", "bx", "lam"):
            add(f"{nm}{o}", 8)
    return lay, col[0]


PP_LAY, NPP = pp_layout()


def pack_pp(inp):
    pp = np.zeros((128, NPP), np.float32)

    def put(name, v):
        c0, n = PP_LAY[name]
        pp[:, c0:c0 + n] = np.asarray(v, np.float32).reshape(n, 128).T

    for l in range(DEPTH):
        put(f"adab{l}", inp["ada_b"][l])
        put(f"lng{l}", inp["ln_g"][l])
        put(f"lnb{l}", inp["ln_b"][l])
    for e in range(2):
        put(f"lbl{e}", inp["a_lb_logits"][e])
        put(f"ang{e}", inp["a_norm_g"][e])
        for i in range(5):
            put(f"mu{e}_{i}", inp["b_mu"][e, i])
        put(f"w0{e}", inp["b_w0"][e])
        put(f"a0{e}", inp["b_a0"][e])
        put(f"kk{e}", inp["b_kk"][e])
        put(f"ka{e}", inp["b_ka"][e])
        put(f"rk{e}", inp["b_rk"][e].reshape(-1))
        put(f"gng{e}", inp["b_gn_g"][e])
        put(f"gnb{e}", inp["b_gn_b"][e])
    put("vmu", inp["b_vmu"][0])
    put("v0", inp["b_v0"][0])
    for o in range(2):
        for j in range(4):
            put(f"cw{o}_{j}", inp["od_conv_w"][o, j])
        put(f"cb{o}", inp["od_conv_b"][o])
        put(f"ba{o}", inp["od_ba"][o].reshape(-1))
        put(f"bx{o}", inp["od_bx"][o].reshape(-1))
        put(f"lam{o}", inp["od_lam"][o])
    return pp


WEIGHT_SPECS = [
    ("ada_w", [DEPTH, D, 3 * D]),
    ("ev_w_in", [2, D, EVC]),
    ("ev_w_out", [2, D, D]),
    ("b_w1", [2, 512, 64]), ("b_w2", [2, 64, 512]),
    ("b_a1", [2, 512, 64]), ("b_a2", [2, 64, 512]),
    ("b_v1", [1, 512, 32]), ("b_v2", [1, 32, 512]),
        ("od_w_in", [2, D, 2 * D]),
    ("od_wa", [2, 4, 256, 256]), ("od_wx", [2, 4, 256, 256]),
    ("od_w_out", [2, D, D]),
]


def build(layers=(0, 1, 2, 3), debug=False):
    nc = bass.Bass("TRN2", target_bir_lowering=False)
    S = Sched(nc)
    with ExitStack() as es:
        def din(name, shape):
            return nc.dram_tensor(name, shape, F32, kind="ExternalInput").ap()

        xT = din("xT", [D, NTOK])
        cTd = din("cT", [128, 16])
        ppd = din("pp", [128, NPP])
        W = {name: din(name, shape) for name, shape in WEIGHT_SPECS}
        outT = nc.dram_tensor("outT", [D, NTOK], F32, kind="ExternalOutput").ap()
        scr = [nc.dram_tensor(f"scr{i}", [D, NTOK], F32).ap() for i in range(2)]
        vfd = nc.dram_tensor("vfd", [512, NTOK], F32).ap()
        dbg = nc.dram_tensor("dbgT", [D, NTOK], BF16, kind="ExternalOutput").ap() if debug else None

        def sb(name, shape, dt=F32):
            return es.enter_context(nc.sbuf_tensor(name, shape, dt))

        pbanks = [es.enter_context(nc.psum_tensor(f"pb{i}", [128, 512], F32)) for i in range(8)]
        pools = {"hps": [0], "hacc": [1], "rproj": [2], "rmm": [3, 4], "racc": [5], "eproj": [6], "stat": [7],
                 "o0": [0, 4], "o1": [1, 5], "o2": [2], "o3": [3], "mm": [3, 4]}
        prr = {k: 0 for k in pools}

        def ps(tag):
            lst = pools[tag]
            i = lst[prr[tag] % len(lst)]
            prr[tag] += 1
            return pbanks[i], f"pb{i}"

        def V(fn, r=(), w=()):
            return S.op("vector", fn, r, w)

        def A(fn, r=(), w=()):
            return S.op("scalar", fn, r, w)

        def G(fn, r=(), w=()):
            return S.op("gpsimd", fn, r, w)

        def PE(fn, r=(), w=()):
            return S.op("tensor", fn, r, w)

        def act(out, in_, func, r, w, scale=1.0, bias=None):
            if bias is None:
                return A(lambda e: e.activation(out=out, in_=in_, func=func, scale=scale), r, w)
            return A(lambda e: e.activation(out=out, in_=in_, func=func, scale=scale, bias=bias), r, w)

        def mm(out, lhsT, rhs, r, w, start=True, stop=True):
            return PE(lambda e: e.matmul(out, lhsT=lhsT, rhs=rhs, start=start, stop=stop), r, w)

        ppt = sb("ppt", [128, NPP])
        S.dma(ppt[:], ppd, "ppt", writes=["ppt"])

        def PP(name, j=None):
            c0, n = PP_LAY[name]
            return ppt[:, c0:c0 + n] if j is None else ppt[:, c0 + j:c0 + j + 1]

        identf = sb("identf", [128, 128])
        identb = sb("identb", [128, 128], BF16)
        ones = sb("ones", [128, 128])
        bo = sb("bo", [128, 128])
        onesb = sb("onesb", [128, 128], BF16)
        bob = sb("bob", [128, 128], BF16)
        bo2 = sb("bo2", [128, 2])
        rm = sb("rm", [128, TT])
        mHf = sb("mHf", [128, 128])
        mk64 = sb("mk64", [64, 3, 64])
        mask512 = sb("mask512", [64, 8, 64], BF16)
        mL2 = sb("mL2", [64, 2, 64], BF16)
        id8 = sb("id8", [64, 8, 64], BF16)
        epsc = sb("epsc", [128, 6])
        G(lambda e: e.memset(identf[:], 1.0), w=["identf"])
        G(lambda e: e.affine_select(out=identf[:], in_=identf[:], pattern=[[-1, 128]], compare_op=ALU.is_equal,
                                    fill=0.0, base=0, channel_multiplier=1), w=["identf"])
        V(lambda e: e.tensor_copy(out=identb[:], in_=identf[:]), r=["identf"], w=["identb"])
        G(lambda e: e.memset(ones[:], 1.0), w=["ones"])
        G(lambda e: e.memset(bo[:], 0.0), w=["bo"])
        G(lambda e: e.memset(bo[0:64, 0:64], 1.0), w=["bo"])
        G(lambda e: e.memset(bo[64:128, 64:128], 1.0), w=["bo"])
        V(lambda e: e.tensor_copy(out=onesb[:], in_=ones[:]), r=["ones"], w=["onesb"])
        V(lambda e: e.tensor_copy(out=bob[:], in_=bo[:]), r=["bo"], w=["bob"])
        G(lambda e: e.memset(bo2[:], 0.0), w=["bo2"])
        G(lambda e: e.memset(bo2[0:64, 0:1], 1.0), w=["bo2"])
        G(lambda e: e.memset(bo2[64:128, 1:2], 1.0), w=["bo2"])
        G(lambda e: e.memset(rm[:], 1.0), w=["rm"])
        G(lambda e: e.memset(rm[:].rearrange("p (c t) -> p c t", t=64)[:, :, 0:1], 0.0), w=["rm"])
        G(lambda e: e.memset(mHf[:], 1.0), w=["mHf"])
        G(lambda e: e.affine_select(out=mHf[:], in_=mHf[:], pattern=[[1, 128]], compare_op=ALU.is_ge,
                                    fill=0.0, base=0, channel_multiplier=-1), w=["mHf"])
        G(lambda e: e.memset(mHf[0:64, 64:128], 0.0), w=["mHf"])
        G(lambda e: e.memset(mk64[:], 1.0), w=["mk64"])
        G(lambda e: e.affine_select(out=mk64[:, 0, :], in_=mk64[:, 0, :], pattern=[[1, 64]], compare_op=ALU.is_ge,
                                    fill=0.0, base=0, channel_multiplier=-1), w=["mk64"])
        G(lambda e: e.affine_select(out=mk64[:, 1, :], in_=mk64[:, 1, :], pattern=[[1, 64]], compare_op=ALU.is_ge,
                                    fill=0.0, base=-1, channel_multiplier=-1), w=["mk64"])
        G(lambda e: e.affine_select(out=mk64[:, 2, :], in_=mk64[:, 2, :], pattern=[[-1, 64]], compare_op=ALU.is_ge,
                                    fill=0.0, base=-1, channel_multiplier=1), w=["mk64"])
        for q in range(8):
            kind = 1 if (q % 2 == 0) else 0
            V(lambda e, q=q, kind=kind: e.tensor_copy(out=mask512[:, q, :], in_=mk64[:, kind, :]), r=["mk64"], w=["mask512"])
            V(lambda e, q=q: e.tensor_copy(out=id8[:, q, :], in_=identf[0:64, 0:64]), r=["identf"], w=["id8"])
        for h in range(2):
            V(lambda e, h=h: e.tensor_copy(out=mL2[:, h, :], in_=mk64[:, 2, :]), r=["mk64"], w=["mL2"])
        G(lambda e: e.memset(epsc[:, 0:1], LN_EPS / (ALPHA * ALPHA)), w=["epsc"])
        G(lambda e: e.memset(epsc[:, 1:2], RMS_EPS), w=["epsc"])
        G(lambda e: e.memset(epsc[:, 2:3], GN_EPS), w=["epsc"])
        G(lambda e: e.memset(epsc[:, 3:4], 1.0), w=["epsc"])
        G(lambda e: e.memset(epsc[:, 4:5], -1.0), w=["epsc"])
        G(lambda e: e.memset(epsc[:, 5:6], 1e-18), w=["epsc"])

        ct = sb("ct", [128, 16])
        cond = sb("cond", [128, 16])
        S.dma(ct[:], cTd, "ct", writes=["ct"])
        act(cond[:], ct[:], AF.Exp, ["ct"], ["cond"], scale=-1.0)
        act(cond[:], cond[:], AF.Ln, ["epsc"], ["cond"], bias=epsc[:, 3:4])
        act(cond[:], cond[:], AF.Exp, [], ["cond"], scale=-1.0)
        V(lambda e: e.tensor_tensor(out=cond[:], in0=cond[:], in1=ct[:], op=ALU.mult), r=["ct"], w=["cond"])

        stage = [sb(f"wst{i}", [128, STG]) for i in range(2)]
        stg_i = [0]

        def stage_load(src, parts, n, view=None):
            i = stg_i[0] % 2
            stg_i[0] += 1
            dst = stage[i][0:parts, 0:n] if view is None else view(stage[i])
            S.dma(dst, src, f"wst{i}", writes=[f"wst{i}"])
            return stage[i], f"wst{i}"

        def cast(eng, out, in_, r, w):
            if eng == "scalar":
                return A(lambda e: e.activation(out=out, in_=in_, func=AF.Copy), r, w)
            return S.op(eng, lambda e: e.tensor_copy(out=out, in_=in_), r, w)

        CE = ("gpsimd", "vector", "scalar")

        modt = sb("modt", [128, DEPTH, 24, 2])
        ms = sb("ms", [128, 2, TT])
        mrow = ms[0:2, :, :].rearrange("p a n -> p (a n)")
        for l in layers:
            mp, mpn = pbanks[0], "pb0"
            mpv = mp[:, 0:48].rearrange("p (j b) -> p j b", b=2)
            awl = W["ada_w"][l].rearrange("(dc p) n -> p dc n", p=128)
            for cgp in range(6):
                pm, pmn = ps("mm")
                for dp in range(4):
                    st, stn = stage_load(awl[:, 2 * dp:2 * dp + 2, cgp * 512:(cgp + 1) * 512], 128, 1024,
                                         view=lambda s_: s_[:, 0:1024].rearrange("p (dc n) -> p dc n", dc=2))
                    sv = st[:, 0:1024].rearrange("p (dc n) -> p dc n", dc=2)
                    for i_ in range(2):
                        dc = 2 * dp + i_
                        mm(pm[0:2, 0:512], cond[:, dc * 2:dc * 2 + 2], sv[:, i_, :], [stn, "cond"], [pmn], start=(dc == 0), stop=(dc == 7))
                act(mrow, pm[0:2, 0:512], AF.Copy, [], [pmn, "mrow"])
                for k_ in range(4):
                    jc = cgp * 4 + k_
                    PE(lambda e, jc=jc, k_=k_, mpv=mpv: e.transpose(out=mpv[:, jc, :], in_=mrow[0:2, k_ * 128:(k_ + 1) * 128], identity=identf[0:2, 0:2]),
                       ["mrow", "identf"], [mpn])
            V(lambda e, l=l, mpv=mpv: e.tensor_tensor(out=modt[:, l], in0=mpv,
                                                      in1=PP(f"adab{l}").unsqueeze(2).to_broadcast([128, 24, 2]), op=ALU.add),
              r=["ppt"], w=[mpn, "modt"])
            V(lambda e, l=l: e.tensor_scalar_add(out=modt[:, l, 8:16, :], in0=modt[:, l, 8:16, :], scalar1=1.0), w=["modt"])
            V(lambda e, l=l: e.tensor_scalar(out=modt[:, l, 16:24, :], in0=modt[:, l, 16:24, :], scalar1=1.0, scalar2=1.0 / ALPHA,
                                             op0=ALU.add, op1=ALU.mult), w=["modt"])

        lbt = sb("lbt", [128, 2, 4])
        omlt = sb("omlt", [128, 2, 4])
        nomlt = sb("nomlt", [128, 2, 4])
        tq = sb("tq", [128, 8, 4])
        l0, l1 = PP("lbl0"), PP("lbl1")
        V(lambda e: e.tensor_tensor(out=tq[:, 0], in0=l0, in1=l1, op=ALU.max), r=["ppt"], w=["tq"])
        V(lambda e: e.tensor_tensor(out=tq[:, 1], in0=l0, in1=tq[:, 0], op=ALU.subtract), r=["ppt", "tq"], w=["tq"])
        V(lambda e: e.tensor_tensor(out=tq[:, 2], in0=l1, in1=tq[:, 0], op=ALU.subtract), r=["ppt", "tq"], w=["tq"])
        act(tq[:, 1:3], tq[:, 1:3], AF.Exp, ["tq"], ["tq"])
        V(lambda e: e.tensor_tensor(out=tq[:, 3], in0=tq[:, 1], in1=tq[:, 2], op=ALU.add), r=["tq"], w=["tq"])
        act(tq[:, 3], tq[:, 3], AF.Ln, [], ["tq"])
        act(tq[:, 3], tq[:, 3], AF.Exp, [], ["tq"], scale=-1.0)
        V(lambda e: e.tensor_tensor(out=tq[:, 4], in0=tq[:, 1], in1=tq[:, 3], op=ALU.mult), r=["tq"], w=["tq"])
        V(lambda e: e.tensor_tensor(out=tq[:, 5], in0=tq[:, 2], in1=tq[:, 3], op=ALU.mult), r=["tq"], w=["tq"])
        V(lambda e: e.tensor_tensor(out=lbt[:, 0], in0=tq[:, 4], in1=tq[:, 4], op=ALU.subtract), r=["tq"], w=["lbt"])
        V(lambda e: e.tensor_tensor(out=tq[:, 6], in0=tq[:, 4], in1=tq[:, 5], op=ALU.add), r=["tq"], w=["tq"])
        V(lambda e: e.tensor_tensor(out=lbt[:, 1], in0=tq[:, 6], in1=tq[:, 4], op=ALU.subtract), r=["tq"], w=["lbt"])
        V(lambda e: e.tensor_scalar(out=omlt[:], in0=lbt[:], scalar1=-1.0, scalar2=1.0, op0=ALU.mult, op1=ALU.add), r=["lbt"], w=["omlt"])
        V(lambda e: e.tensor_scalar(out=nomlt[:], in0=lbt[:], scalar1=1.0, scalar2=-1.0, op0=ALU.mult, op1=ALU.add), r=["lbt"], w=["nomlt"])

        c8t = sb("c8t", [128, 2, 8])
        tl = sb("tl", [128, 6, 8])
        for o in range(2):
            lam = PP(f"lam{o}")
            act(tl[:, 0], lam, AF.Exp, ["ppt"], ["tl"], scale=-1.0)
            V(lambda e: e.tensor_scalar(out=tl[:, 1], in0=tl[:, 0], scalar1=-0.2, scalar2=0.25, op0=ALU.mult, op1=ALU.add), r=["tl"], w=["tl"])
            for cst in (1.0 / 3.0, 0.5, 1.0):
                V(lambda e: e.tensor_tensor(out=tl[:, 1], in0=tl[:, 1], in1=tl[:, 0], op=ALU.mult), r=["tl"], w=["tl"])
                V(lambda e, cst=cst: e.tensor_scalar(out=tl[:, 1], in0=tl[:, 1], scalar1=-1.0, scalar2=cst, op0=ALU.mult, op1=ALU.add), r=["tl"], w=["tl"])
            V(lambda e: e.tensor_tensor(out=tl[:, 1], in0=tl[:, 1], in1=tl[:, 0], op=ALU.mult), r=["tl"], w=["tl"])
            act(tl[:, 2], tl[:, 0], AF.Ln, ["tl", "epsc"], ["tl"], bias=epsc[:, 3:4])
            V(lambda e: e.tensor_single_scalar(out=tl[:, 3], in_=tl[:, 0], scalar=0.05, op=ALU.is_lt), r=["tl"], w=["tl"])
            V(lambda e: e.tensor_tensor(out=tl[:, 4], in0=tl[:, 1], in1=tl[:, 2], op=ALU.subtract), r=["tl"], w=["tl"])
            V(lambda e: e.tensor_tensor(out=tl[:, 4], in0=tl[:, 4], in1=tl[:, 3], op=ALU.mult), r=["tl"], w=["tl"])
            V(lambda e: e.tensor_tensor(out=tl[:, 4], in0=tl[:, 4], in1=tl[:, 2], op=ALU.add), r=["tl"], w=["tl"])
            V(lambda e, o=o: e.tensor_scalar_mul(out=c8t[:, o], in0=tl[:, 4], scalar1=-8.0), r=["tl"], w=["c8t"])
        omka = sb("omka", [128, 2, 4])
        for e_ in range(2):
            V(lambda e, e_=e_: e.tensor_scalar(out=omka[:, e_], in0=PP(f"ka{e_}"), scalar1=-1.0, scalar2=1.0, op0=ALU.mult, op1=ALU.add),
              r=["ppt"], w=["omka"])

        NEG = {}
        ncol = 0
        for nm, n_ in (("w00", 4), ("w01", 4), ("a00", 4), ("a01", 4), ("v0", 4), ("ba0", 8), ("ba1", 8), ("bx0", 8), ("bx1", 8)):
            NEG[nm] = (ncol, n_)
            ncol += n_
        negb = sb("negb", [128, ncol])
        for nm, (c0_, n_) in NEG.items():
            V(lambda e, nm=nm, c0_=c0_, n_=n_: e.tensor_scalar_mul(out=negb[:, c0_:c0_ + n_], in0=PP(nm), scalar1=-1.0), r=["ppt"], w=["negb"])

        def NB_(nm, j):
            c0_, _ = NEG[nm]
            return negb[:, c0_ + j:c0_ + j + 1]

        wbig = sb("wbig", [128, 8 * EVC], BF16)
        woutb = sb("woutb", [128, 8, D], BF16)
        xts = [sb(f"xt{i}", [128, 8, TT]) for i in range(2)]
        hb = sb("hb", [128, 8, TT], BF16)
        mixeds = [sb(f"mixed{i}", [128, 8, TT], BF16) for i in range(2)]
        mut = ms[:, 0, :]
        sqt = [sb("sqt0", [128, TT])[:], ms[:, 1, :]]
        rst = sb("rst", [128, TT])

        vxo = sb("vxo", [128, TT])
        vft = sb("vft", [128, TT])
        lora = sb("lora", [128, 2176], BF16)
        fsc = sb("fsc", [128, 4])
        RWORDS = 15990
        Rr = sb("Rr", [128, RWORDS])
        roff = [0]

        def carve(shape, dt=F32, parts=128):
            n = 1
            for d_ in shape[1:]:
                n *= d_
            words = n if dt == F32 else (n + 1) // 2
            a = roff[0]
            roff[0] += words
            assert roff[0] <= RWORDS, f"region overflow {roff[0]} > {RWORDS}"
            ap = Rr[0:shape[0], a:a + words]
            if dt != F32:
                ap = ap.bitcast(dt)
            if len(shape) > 2:
                names = " ".join(f"d{i}" for i in range(1, len(shape)))
                kw = {f"d{i}": shape[i] for i in range(1, len(shape))}
                ap = ap.rearrange(f"p ({names}) -> p {names}", **kw)
            return ap

        import types
        c = types.SimpleNamespace(**dict(locals()))
        c.final_ops = []
        emit_layers(c)
        S.emit(final_wait_ops=c.final_ops)
    return nc, S


def _w(el):
    return len(el) if isinstance(el, list) else 1


DUR = {"tensor": 0.16, "vector": 0.42, "scalar": 0.40, "gpsimd": 0.35, "sync": 2.0}
XLAT = 0.30
SLAT = 0.12


def merge_threads(lists, prio=None):
    if prio is None:
        prio = [0.0] * len(lists)
    prio = [p for p, l in zip(prio, lists) if l]
    lists = [l for l in lists if l]
    if len(lists) <= 1:
        return flat(lists[0]) if lists else []
    eng_free = {}
    last_w = {}
    readers = {}
    out = []
    pos = [0] * len(lists)

    def est_start(op):
        eng, _, reads, writes = op[0], op[1], op[2], op[3]
        t = eng_free.get(eng, 0.0)
        for b in reads:
            w = last_w.get(b)
            if w is not None:
                t = max(t, w[0] + (XLAT if w[1] != eng else SLAT))
        for b in writes:
            w = last_w.get(b)
            if w is not None:
                t = max(t, w[0] + (XLAT if w[1] != eng else (0.0 if eng == "tensor" else SLAT)))
            for re_, rf in readers.get(b, {}).items():
                if re_ != eng:
                    t = max(t, rf + XLAT)
        return t

    def commit(op):
        eng, reads, writes = op[0], op[2], op[3]
        s = est_start(op)
        f = s + DUR.get(eng, 0.4)
        eng_free[eng] = f if eng != "sync" else s + 0.05
        for b in reads:
            readers.setdefault(b, {})[eng] = f
        for b in writes:
            last_w[b] = (f, eng)
            readers[b] = {}
        out.append(op)

    n_el = sum(len(l) for l in lists)
    k = 0
    while k < n_el:
        best, bt = None, None
        for i, l in enumerate(lists):
            if pos[i] < len(l):
                el = l[pos[i]]
                t = est_start(el[0] if isinstance(el, list) else el) - prio[i]
                if bt is None or t < bt - 1e-9:
                    best, bt = i, t
        el = lists[best][pos[best]]
        pos[best] += 1
        k += 1
        if isinstance(el, list):
            for op in el:
                commit(op)
        else:
            commit(el)
    return out


def flat(lst):
    out = []
    for el in lst:
        if isinstance(el, list):
            out.extend(el)
        else:
            out.append(el)
    return out


def types_ns():
    import types
    return types.SimpleNamespace()


def emit_layers(c):
    S, nc = c.S, c.nc
    V, A, G, PE, act, mm, ps, sb, PP = c.V, c.A, c.G, c.PE, c.act, c.mm, c.ps, c.sb, c.PP
    hb, xts, modt, wbig, woutb = c.hb, c.xts, c.modt, c.wbig, c.woutb
    layers = list(c.layers)
    ONE, MONE, TINY = c.epsc[:, 3:4], c.epsc[:, 4:5], c.epsc[:, 5:6]

    def sigm(buf, bn, src, r, w_extra=(), nbias=None, xscale=1.0):
        if nbias is None:
            act(buf, src, AF.Exp, r, list(w_extra) + [bn], scale=-xscale)
        else:
            act(buf, src, AF.Exp, list(r) + ["negb"], list(w_extra) + [bn], scale=-xscale, bias=nbias)
        act(buf, buf, AF.Ln, ["epsc"], [bn], bias=ONE[0:buf.shape[0], :])
        act(buf, buf, AF.Exp, [], [bn], scale=-1.0)
    c.sigm = sigm

    def xsrc_dst(li):
        src = c.xT if li == 0 else c.scr[(li - 1) % 2]
        dst = c.outT if li == len(layers) - 1 else c.scr[li % 2]
        return src, dst

    def load_x(li, b, ti, slot):
        src, _ = xsrc_dst(li)
        tok0 = b * T + ti * TT
        S.dma(xts[slot][:], src.rearrange("(dc p) n -> p dc n", p=128)[:, :, tok0:tok0 + TT], f"xt{slot}",
              reads=[f"dram{li}.{b}.{ti}"], writes=[f"xt{slot}.{dc}" for dc in range(8)])

    def load_weights(l):
        e_ = l // 2
        if l % 2 == 0:
            for dc in range(8):
                for pc in range(4):
                    st, stn = c.stage_load(c.W["ev_w_in"][e_, dc * 128:(dc + 1) * 128, pc * 1152:(pc + 1) * 1152], 128, 1152)
                    c.cast(c.CE[dc % 3], wbig[:, dc * EVC + pc * 1152: dc * EVC + (pc + 1) * 1152], st[:, 0:1152], [stn], [f"wbig.{dc}"])
            wo = c.W["ev_w_out"][e_]
        else:
            for dc in range(8):
                for pc in range(2):
                    st, stn = c.stage_load(c.W["od_w_in"][e_, dc * 128:(dc + 1) * 128, pc * 1024:(pc + 1) * 1024], 128, 1024)
                    c.cast(c.CE[dc % 3], wbig[:, dc * 2048 + pc * 1024:dc * 2048 + (pc + 1) * 1024], st[:, 0:1024], [stn], [f"wbig.{dc}"])
            for wi, wn in enumerate(("od_wa", "od_wx")):
                for g in range(4):
                    st, stn = c.stage_load(c.W[wn][e_, g].rearrange("(kc p) n -> p kc n", p=128), 128, 512,
                                           view=lambda s: s[:, 0:512].rearrange("p (kc n) -> p kc n", kc=2))
                    off = 16384 + wi * 2048 + g * 512
                    c.cast(c.CE[g % 3], wbig[:, off:off + 512], st[:, 0:512], [stn], ["wgate"])
            wo = c.W["od_w_out"][e_]
        for cc in range(8):
            st, stn = c.stage_load(wo[cc * 128:(cc + 1) * 128, :], 128, 1024)
            c.cast(c.CE[cc % 3], woutb[:, cc, :], st[:, 0:1024], [stn], [f"wout.{cc}"])

    def proj_fm(wv, col0, pool):
        pa, pn = ps(pool)
        for dc in range(8):
            mm(pa[:, 0:TT], wv[:, dc, col0:col0 + 128], hb[:, dc, :], [f"wbig.{dc}", f"hb.{dc}"], [pn],
               start=(dc == 0), stop=(dc == 7))
        return pa[:, 0:TT], pn
    c.proj_fm = proj_fm

    def tile_prologue(l, b, xt, slot):
        for dc in range(8):
            eng = "vector"
            S.op(eng, lambda e, dc=dc: e.tensor_scalar(out=hb[:, dc, :], in0=xt[:, dc, :],
                                                       scalar1=modt[:, l, 8 + dc, b:b + 1], scalar2=modt[:, l, dc, b:b + 1],
                                                       op0=ALU.mult, op1=ALU.add),
                 [f"xt{slot}.{dc}", "modt"], [f"hb.{dc}"])

    def tile_epilogue(li, l, b, ti, xt, slot, mixed, mxn):
        if c.debug and li == len(layers) - 1:
            tok0_ = b * T + ti * TT
            S.dma(c.dbg.rearrange("(dc p) n -> p dc n", p=128)[:, :, tok0_:tok0_ + TT], mixed[:], "dbgst",
                  reads=[f"{mxn}.{cc}" for cc in range(8)], writes=["dbgdram"])
        st_p, st_n = ps("stat")
        for dc in range(8):
            pa, pn = ps("eproj")
            for cc in range(8):
                mm(pa[:, 0:TT], woutb[:, cc, dc * 128:(dc + 1) * 128], mixed[:, cc, :], [f"wout.{cc}", f"{mxn}.{cc}"], [pn],
                   start=(cc == 0), stop=(cc == 7))
            V(lambda e, dc=dc, pa=pa: e.scalar_tensor_tensor(out=xt[:, dc, :], in0=pa[:, 0:TT], scalar=modt[:, l, 16 + dc, b:b + 1],
                                                            in1=xt[:, dc, :], op0=ALU.mult, op1=ALU.add),
              r=["modt"], w=[pn, f"xt{slot}.{dc}"])
            if dc == 0:
                act(c.sqt[1], xt[:, 0, :], AF.Square, [f"xt{slot}.0"], ["sqt1"])
            else:
                act(c.sqt[0], xt[:, dc, :], AF.Square, [f"xt{slot}.{dc}"], ["sqt0"])
                G(lambda e: e.tensor_tensor(out=c.sqt[1], in0=c.sqt[1], in1=c.sqt[0], op=ALU.add), r=["sqt0"], w=["sqt1"])
                if dc == 1:
                    G(lambda e: e.tensor_tensor(out=c.mut, in0=xt[:, 0, :], in1=xt[:, 1, :], op=ALU.add), r=[f"xt{slot}.0", f"xt{slot}.1"], w=["mut"])
                else:
                    G(lambda e, dc=dc: e.tensor_tensor(out=c.mut, in0=c.mut, in1=xt[:, dc, :], op=ALU.add), r=[f"xt{slot}.{dc}"], w=["mut"])
        mm(st_p[:, 0:2 * TT], c.ones[:], c.ms[:].rearrange("p a n -> p (a n)"), ["ones", "mut", "sqt1"], [st_n])
        mut, rst = c.mut, c.rst
        act(mut, st_p[:, 0:TT], AF.Copy, [], [st_n, "mut"], scale=1.0 / D)
        act(rst[:], mut, AF.Square, ["mut"], ["rst"])
        V(lambda e: e.scalar_tensor_tensor(out=rst[:], in0=st_p[:, TT:2 * TT], scalar=1.0 / D, in1=rst[:], op0=ALU.mult, op1=ALU.subtract),
          w=[st_n, "rst"])
        act(rst[:], rst[:], AF.Ln, ["epsc"], ["rst"], bias=c.epsc[:, 0:1])
        act(rst[:], rst[:], AF.Exp, [], ["rst"], scale=-0.5)
        for dc in range(8):
            V(lambda e, dc=dc: e.tensor_tensor(out=xt[:, dc, :], in0=xt[:, dc, :], in1=mut, op=ALU.subtract), r=["mut"], w=[f"xt{slot}.{dc}"])
        for dc in range(8):
            V(lambda e, dc=dc: e.tensor_tensor(out=xt[:, dc, :], in0=xt[:, dc, :], in1=rst[:], op=ALU.mult), r=["rst"], w=[f"xt{slot}.{dc}"])
        for dc in range(8):
            G(lambda e, dc=dc: e.tensor_scalar(out=xt[:, dc, :], in0=xt[:, dc, :], scalar1=PP(f"lng{l}", dc), scalar2=PP(f"lnb{l}", dc),
                                               op0=ALU.mult, op1=ALU.add), r=["ppt"], w=[f"xt{slot}.{dc}"])
        _, dst = xsrc_dst(li)
        tok0 = b * T + ti * TT
        o = S.dma(dst.rearrange("(dc p) n -> p dc n", p=128)[:, :, tok0:tok0 + TT], xt[:], f"xst{slot}",
                  reads=[f"xt{slot}.{dc}" for dc in range(8)], writes=[f"dram{li + 1}.{b}.{ti}"])
        if li == len(layers) - 1:
            c.final_ops.append(o)

    od = types_ns()

    def alloc_odd():
        c.roff[0] = 0
        od.xcr = c.carve([128, 8, TT + 3])
        od.xc = c.carve([128, 8, TT])
        od.xcb = c.carve([128, 8, TT], BF16)
        od.sgc = c.carve([128, 8, TT])
        od.gr = c.carve([128, 8, TT])
        od.gi = c.carve([128, 8, TT])
        od.at = c.carve([128, 8, TT])
        od.hc = [c.carve([128, TT]) for i in range(2)]
        od.hst = c.carve([128, 8])
        print("odd region words", c.roff[0])

    def odd_tile(l, b, ti, xt, slot, mixed, mxn, g):
        o_ = l // 2
        pool = f"o{g}"
        CS = (2 * g, 2 * g + 1)
        wv = wbig[:, 0:16384].rearrange("p (dc n) -> p dc n", dc=8)
        wg = wbig[:, 16384:16384 + 4096].rearrange("p (w g kc n) -> p w g kc n", w=2, g=4, kc=2)
        xcr, xc, xcb, sgc, gr, gi, at, hst = od.xcr, od.xc, od.xcb, od.sgc, od.gr, od.gi, od.at, od.hst
        hn = f"hst.{g}"
        if ti == 0:
            G(lambda e: e.memset(xcr[:, 2 * g:2 * g + 2, 0:3], 0.0), w=[f"xcr.{cc}" for cc in CS])
            G(lambda e: e.memset(hst[:, 2 * g:2 * g + 2], 0.0), w=[hn])
        for cc in CS:
            pa, pn = proj_fm(wv, cc * 128, pool)
            act(xcr[:, cc, 3:3 + TT], pa, AF.Copy, [], [pn, f"xcr.{cc}"])
        for cc in CS:
            V(lambda e, cc=cc: e.tensor_scalar(out=xc[:, cc, :], in0=xcr[:, cc, 3:3 + TT], scalar1=PP(f"cw{o_}_3", cc), scalar2=PP(f"cb{o_}", cc),
                                               op0=ALU.mult, op1=ALU.add), r=[f"xcr.{cc}", "ppt"], w=[f"xc.{cc}"])
        for j in (2, 1, 0):
            for cc in CS:
                V(lambda e, cc=cc, j=j: e.scalar_tensor_tensor(out=xc[:, cc, :], in0=xcr[:, cc, j:j + TT], scalar=PP(f"cw{o_}_{j}", cc),
                                                               in1=xc[:, cc, :], op0=ALU.mult, op1=ALU.add),
                  r=[f"xcr.{cc}", "ppt"], w=[f"xc.{cc}"])
        for cc in CS:
            V(lambda e, cc=cc: e.tensor_copy(out=xcr[:, cc, 0:3], in_=xcr[:, cc, TT:TT + 3]), w=[f"xcr.{cc}"])
            G(lambda e, cc=cc: e.tensor_copy(out=xcb[:, cc, :], in_=xc[:, cc, :]), r=[f"xc.{cc}"], w=[f"xcb.{cc}"])
        for cc in CS:
            pa, pn = proj_fm(wv, 1024 + cc * 128, pool)
            act(sgc[:, cc, :], pa, AF.Exp, [], [pn, f"sgc.{cc}"], scale=-1.0)
            act(at[:, cc, :], pa, AF.Copy, [], [pn, f"at.{cc}"])
        for cc in CS:
            act(sgc[:, cc, :], sgc[:, cc, :], AF.Ln, ["epsc"], [f"sgc.{cc}"], bias=ONE)
        for cc in CS:
            act(sgc[:, cc, :], sgc[:, cc, :], AF.Exp, [], [f"sgc.{cc}"], scale=-1.0)
        for cc in CS:
            V(lambda e, cc=cc: e.tensor_tensor(out=sgc[:, cc, :], in0=sgc[:, cc, :], in1=at[:, cc, :], op=ALU.mult), r=[f"at.{cc}"], w=[f"sgc.{cc}"])
        for oc in CS:
            jh = oc % 2
            for wi, dst, bn in ((0, gr, f"ba{o_}"), (1, gi, f"bx{o_}")):
                pa, pn = ps(pool)
                for kc in range(2):
                    mm(pa[:, 0:TT], wg[:, wi, g, kc, jh * 128:(jh + 1) * 128], xcb[:, g * 2 + kc, :], ["wgate", f"xcb.{g * 2 + kc}"], [pn],
                       start=(kc == 0), stop=(kc == 1))
                act(dst[:, oc, :], pa[:, 0:TT], AF.Exp, ["negb"], [pn, f"{'gr' if wi == 0 else 'gi'}.{oc}"], scale=-1.0, bias=c.NB_(bn, oc))
        for nm, dst in (("gr", gr), ("gi", gi)):
            for oc in CS:
                act(dst[:, oc, :], dst[:, oc, :], AF.Ln, ["epsc"], [f"{nm}.{oc}"], bias=ONE)
            for oc in CS:
                act(dst[:, oc, :], dst[:, oc, :], AF.Exp, [], [f"{nm}.{oc}"], scale=-1.0)
        for oc in CS:
            act(at[:, oc, :], gr[:, oc, :], AF.Exp, [f"gr.{oc}", "c8t"], [f"at.{oc}"], scale=c.c8t[:, o_, oc:oc + 1])
        for oc in CS:
            act(gr[:, oc, :], at[:, oc, :], AF.Square, [f"at.{oc}"], [f"gr.{oc}"])
        for oc in CS:
            V(lambda e, oc=oc: e.tensor_scalar(out=gr[:, oc, :], in0=gr[:, oc, :], scalar1=-1.0, scalar2=1.0, op0=ALU.mult, op1=ALU.add), w=[f"gr.{oc}"])
        for oc in CS:
            act(gr[:, oc, :], gr[:, oc, :], AF.Ln, ["epsc"], [f"gr.{oc}"], bias=TINY)
        for oc in CS:
            act(gr[:, oc, :], gr[:, oc, :], AF.Exp, [], [f"gr.{oc}"], scale=0.5)
        for oc in CS:
            V(lambda e, oc=oc: e.tensor_tensor(out=gi[:, oc, :], in0=gi[:, oc, :], in1=gr[:, oc, :], op=ALU.mult), r=[f"gr.{oc}"], w=[f"gi.{oc}"])
        for oc in CS:
            V(lambda e, oc=oc: e.tensor_tensor(out=gi[:, oc, :], in0=gi[:, oc, :], in1=xc[:, oc, :], op=ALU.mult), r=[f"xc.{oc}"], w=[f"gi.{oc}"])
        for oc in CS:
            V(lambda e, oc=oc: e.tensor_tensor_scan(out=gr[:, oc, :], data0=at[:, oc, :], data1=gi[:, oc, :], initial=hst[:, oc:oc + 1],
                                                    op0=ALU.mult, op1=ALU.add), r=[f"at.{oc}", f"gi.{oc}", hn], w=[f"gr.{oc}"])
        V(lambda e: e.tensor_copy(out=hst[:, 2 * g:2 * g + 2], in_=gr[:, 2 * g:2 * g + 2, TT - 1]), r=[f"gr.{oc}" for oc in CS], w=[hn])
        for oc in CS:
            V(lambda e, oc=oc: e.tensor_tensor(out=mixed[:, oc, :], in0=gr[:, oc, :], in1=sgc[:, oc, :], op=ALU.mult), r=[f"gr.{oc}", f"sgc.{oc}"], w=[f"{mxn}.{oc}"])

    have_odd = any(l % 2 == 1 for l in layers)
    have_even = any(l % 2 == 0 for l in layers)
    if have_odd:
        alloc_odd()
    if have_even:
        ev = alloc_even(c)
    tile_ctr = 0
    for li, l in enumerate(layers):
        fsc = c.fsc
        S.fence({"vector": lambda e: e.memset(fsc[:, 0:1], 0.0), "scalar": lambda e: e.activation(out=fsc[:, 1:2], in_=c.epsc[:, 0:1], func=AF.Copy),
                 "gpsimd": lambda e: e.memset(fsc[:, 2:3], 0.0)})
        load_weights(l)
        if l % 2 == 0:
            even_layer_setup(c, ev, l)
        seq = [(b, ti) for b in range(NB) for ti in range(NTILE)]
        load_x(li, seq[0][0], seq[0][1], tile_ctr % 2)
        if len(seq) > 1:
            load_x(li, seq[1][0], seq[1][1], (tile_ctr + 1) % 2)
        prevE = []
        for k, (b, ti) in enumerate(seq):
            slot = tile_ctr % 2
            xt = xts[slot]
            P = S.record(lambda: tile_prologue(l, b, xt, slot))
            mixed, mxn = c.mixeds[tile_ctr % 2], f"mixed{tile_ctr % 2}"
            if l % 2 == 1:
                M = merge_threads([S.record(lambda g=g: odd_tile(l, b, ti, xt, slot, mixed, mxn, g)) for g in range(4)], prio=[0.0, 0.6, 1.2, 1.8])
            else:
                Mh = S.record(lambda: hgrn2_tile(c, ev, l, b, ti, mixed, mxn))
                Mr = S.record(lambda: rwkv_tile(c, ev, l, b, ti, mixed, mxn))
                M = merge_threads([Mh, Mr], prio=[0.0, 0.8])
            E = S.record(lambda: tile_epilogue(li, l, b, ti, xt, slot, mixed, mxn))
            S.replay(merge_threads([prevE, P + M], prio=[0.0, 0.8]))
            if k >= 1 and k + 1 < len(seq):
                load_x(li, seq[k + 1][0], seq[k + 1][1], (tile_ctr + 1) % 2)
            prevE = E
            tile_ctr += 1
        S.replay(prevE)
def make_in_maps(inp, x_full):
    pp = pack_pp(inp)
    maps = []
    shared = {name: np.ascontiguousarray(np.asarray(inp[name], np.float32)) for name, _ in WEIGHT_SPECS}
    for i in range(NCORE):
        xs = np.asarray(x_full[NB * i:NB * (i + 1)], np.float32).reshape(NTOK, D)
        xTc = np.ascontiguousarray(xs.T)
        cs = np.asarray(inp["c"][NB * i:NB * (i + 1)], np.float32)
        cT = np.ascontiguousarray(cs.reshape(NB, 8, 128).transpose(2, 1, 0).reshape(128, 16))
        m = {"xT": xTc, "cT": cT, "pp": pp}
        m.update(shared)
        maps.append(m)
    return maps


_CACHE = {}


def run_layers(inp, x_full, layers=(0, 1, 2, 3), debug=False):
    key = (tuple(layers), debug)
    if key not in _CACHE:
        _CACHE[key] = build(layers=layers, debug=debug)
    nc, S = _CACHE[key]
    maps = make_in_maps(inp, x_full)
    res = run_bass_kernel_spmd(nc, maps, core_ids=list(range(NCORE)))
    outs = []
    for i in range(NCORE):
        o = np.asarray(res.results[i]["outT"])
        outs.append(o.T.reshape(NB, T, D))
    if debug:
        res.dbg = np.asarray(res.results[0]["dbgT"]).astype(np.float32).T.reshape(NB, T, D)
    return np.concatenate(outs, axis=0), res


def kernel(**inputs):
    out, _ = run_layers(inputs, inputs["x"])
    return out.astype(np.float32)


def alloc_even(c):
    ev = types_ns()
    c.roff[0] = 0
    cv = c.carve
    ev.hg = [cv([128, TT]) for _ in range(7)]
    ev.rg = [cv([128, TT + 1]) for _ in range(6)]
    ev.itok = cv([128, NTB, 512], BF16)
    ev.Qb = cv([128, TT], BF16)
    ev.Kend = cv([128, TT], BF16)
    ev.Qmid = cv([128, TT], BF16)
    ev.Kmid = cv([128, TT], BF16)
    ev.KendT = cv([128, NTB, 128], BF16)
    ev.PT = cv([128, NTB, 128], BF16)
    ev.sm = cv([128, 3, NCH])
    ev.Sf = cv([128, 4, 128])
    ev.Sb = cv([128, 4, 2, 128], BF16)
    ev.zc = cv([128, 4])
    ev.zw = cv([128, 4, TT], BF16)
    ev.za = cv([128, 4, TT], BF16)
    ev.zv = cv([128, 4, TT], BF16)
    ev.hid = cv([64, 3, TT], BF16)
    ev.hidf = cv([64, TT])
    ev.lwt = cv([128, TT])
    ev.icl = cv([128, TT])
    ev.vg = cv([128, TT])
    ev.cr = cv([128, 3, 4])
    ev.raw = cv([128, TT + 1])
    ev.kh = cv([128, TT])
    ev.cw = cv([128, TT])
    ev.EW = cv([128, TT])
    ev.rkp = cv([128, TT])
    ev.ARz = cv([128, 2, NCH, 2, 64], BF16)
    ev.Bt = cv([128, TT], BF16)
    ev.Kt = cv([128, TT], BF16)
    ev.Ns = cv([64, NCH, 8, 64], BF16)
    ev.NT0 = cv([64, NCH * 2, 64], BF16)
    ev.LA = cv([64, 2, 8, 64], BF16)
    ev.LB = cv([64, 2, 8, 64], BF16)
    ev.Pq = cv([64, 8, 64], BF16)
    ev.Vtb = cv([64, NCH, 128], BF16)
    ev.BKT = cv([64, NCH, 2, 128], BF16)
    ev.Yall = cv([64, NCH, 128])
    ev.Ysq = cv([64, NCH, 128])
    ev.Xsb = [cv([64, 128], BF16) for _ in range(2)]
    ev.Usb = [cv([64, 128], BF16) for _ in range(2)]
    ev.st = cv([64, 4, 8])
    ev.Hs = cv([128, 4, 64])
    ev.Hb = cv([128, 4, 64], BF16)
    ev.sgb = cv([128, TT])
    ev.wprev = cv([128, 4])
    print("even region words", c.roff[0])
    return ev


def even_layer_setup(c, ev, l):
    e_ = l // 2
    S, V, G = c.S, c.V, c.G
    lora = c.lora
    specs = [("b_w1", e_, 0, 64), ("b_a1", e_, 256, 64)]
    if e_ == 1:
        specs.append(("b_v1", 0, 512, 32))
    for nm, idx, off, r_ in specs:
        st, stn = c.stage_load(c.W[nm][idx].rearrange("(j p) n -> p j n", p=128), 128, 4 * r_,
                               view=lambda s, r_=r_: s[:, 0:4 * r_].rearrange("p (j n) -> p j n", j=4))
        c.cast("vector", lora[:, off:off + 4 * r_], st[:, 0:4 * r_], [stn], ["lora"])
    specs2 = [("b_w2", e_, 640, 64), ("b_a2", e_, 1152, 64)]
    if e_ == 1:
        specs2.append(("b_v2", 0, 1664, 32))
    for nm, idx, off, r_ in specs2:
        st, stn = c.stage_load(c.W[nm][idx], r_, 512)
        c.cast("vector", lora[0:r_, off:off + 512], st[0:r_, 0:512], [stn], ["lora"])
    G(lambda e: e.memset(ev.ARz[:], 0.0), w=["ARz"])


def hgrn2_tile(c, ev, l, b, ti, mixed, mxn):
    S, V, A, G, PE, act, mm, ps, PP, sigm = c.S, c.V, c.A, c.G, c.PE, c.act, c.mm, c.ps, c.PP, c.sigm
    hb, wbig, proj_fm = c.hb, c.wbig, c.proj_fm
    e_ = l // 2
    wv = wbig[:].rearrange("p (dc n) -> p dc n", dc=8)
    C3 = lambda ap: ap.rearrange("p (c t) -> p c t", t=64)
    ONE = c.epsc[:, 3:4]

    def tr(out, in_, ident, r, w):
        return PE(lambda e: e.transpose(out=out, in_=in_, identity=ident), r, w)

    itok, Qb, Kend, Qmid, Kmid, KendT, PT, sm, Sf, Sb = ev.itok, ev.Qb, ev.Kend, ev.Qmid, ev.Kmid, ev.KendT, ev.PT, ev.sm, ev.Sf, ev.Sb
    for tb in range(NTB):
        pa, pn = ps("hps")
        for dc in range(8):
            mm(pa[:, 0:512], hb[:, dc, tb * 128:(tb + 1) * 128], wv[:, dc, 1024:1536], [f"hb.{dc}", f"wbig.{dc}"], [pn],
               start=(dc == 0), stop=(dc == 7))
        act(itok[:, tb, :], pa[:, 0:512], AF.Copy, [], [pn, f"itok.{tb}"])
    if ti == 0:
        G(lambda e: e.memset(Sf[:], 0.0), w=["Sf"])
        G(lambda e: e.memset(Sb[:], 0.0), w=["Sb"])
    q, sg, gs, sgl, kk, bc, ee = ev.hg
    qn, sgn, gsn, sgln, kkn_, bcn, een = [f"hg{i}" for i in range(7)]
    for hd in range(4):
        pq, pqn = proj_fm(wv, hd * 128, "hps")
        act(q[:], pq, AF.Copy, [], [pqn, qn])
        pf, pfn = proj_fm(wv, 512 + hd * 128, "hps")
        sigm(sg[:], sgn, pf, [], w_extra=[pfn])
        pg, pgn = proj_fm(wv, 1536 + hd * 128, "hps")
        act(gs[:], pg, AF.Copy, [], [pgn, gsn])
        sigm(sgl[:], sgln, pg, [], w_extra=[pgn])
        V(lambda e, hd=hd: e.tensor_scalar(out=kk[:], in0=sg[:], scalar1=c.nomlt[:, e_, hd:hd + 1], scalar2=c.omlt[:, e_, hd:hd + 1],
                                           op0=ALU.mult, op1=ALU.add), r=[sgn, "nomlt", "omlt"], w=[kkn_])
        V(lambda e, hd=hd: e.tensor_scalar(out=sg[:], in0=sg[:], scalar1=c.omlt[:, e_, hd:hd + 1], scalar2=c.lbt[:, e_, hd:hd + 1],
                                           op0=ALU.mult, op1=ALU.add), r=["omlt", "lbt"], w=[sgn])
        act(sg[:], sg[:], AF.Ln, [], [sgn])
        V(lambda e: e.tensor_tensor_scan(out=bc[:], data0=c.rm[:], data1=sg[:], initial=0.0, op0=ALU.mult, op1=ALU.add), r=["rm", sgn], w=[bcn])
        bc3 = C3(bc[:])
        act(ee[:], bc[:], AF.Exp, [bcn], [een])
        V(lambda e: e.tensor_tensor(out=Qb[:], in0=q[:], in1=ee[:], op=ALU.mult), r=[qn, een], w=["Qb"])
        V(lambda e: e.tensor_tensor(out=C3(sg[:]), in0=bc3[:, :, 63:64].to_broadcast([128, NCH, 64]), in1=bc3, op=ALU.subtract), r=[bcn], w=[sgn])
        act(sg[:], sg[:], AF.Exp, [], [sgn])
        V(lambda e: e.tensor_tensor(out=Kend[:], in0=kk[:], in1=sg[:], op=ALU.mult), r=[kkn_, sgn], w=["Kend"])
        act(sm[:, 0, :], bc3[:, :, 31], AF.Exp, [bcn], ["sm"], scale=-1.0)
        V(lambda e: e.tensor_tensor(out=sm[:, 1, :], in0=bc3[:, :, 31], in1=bc3[:, :, 63], op=ALU.subtract), r=[bcn], w=["sm"])
        act(sm[:, 1, :], sm[:, 1, :], AF.Exp, [], ["sm"])
        act(sm[:, 2, :], bc3[:, :, 63], AF.Exp, [bcn], ["sm"])
        V(lambda e: e.tensor_tensor(out=C3(Qmid[:]), in0=C3(Qb[:]), in1=sm[:, 0, :].unsqueeze(2).to_broadcast([128, NCH, 64]), op=ALU.mult),
          r=["Qb", "sm"], w=["Qmid"])
        V(lambda e: e.tensor_tensor(out=C3(Kmid[:]), in0=C3(Kend[:]), in1=sm[:, 1, :].unsqueeze(2).to_broadcast([128, NCH, 64]), op=ALU.mult),
          r=["Kend", "sm"], w=["Kmid"])
        V(lambda e: e.tensor_tensor(out=gs[:], in0=gs[:], in1=sgl[:], op=ALU.mult), r=[sgln], w=[gsn])
        for tb in range(NTB):
            ts_ = slice(tb * 128, (tb + 1) * 128)
            ptr, ptrn = ps("hps")
            ptb = ptr[:, 0:64].bitcast(BF16)
            tr(ptb, Kend[:, ts_], c.identb[:], ["Kend", "identb"], [ptrn])
            act(KendT[:, tb, :], ptb, AF.Copy, [], [ptrn, "KendT"])
            psc, pscn = ps("hps")
            mm(psc[:, 0:128], Kmid[:, ts_], Qmid[:, ts_], ["Kmid", "Qmid"], [pscn])
            V(lambda e, tb=tb, psc=psc: e.tensor_tensor(out=PT[:, tb, :], in0=psc[:, 0:128], in1=c.mHf[:], op=ALU.mult), r=["mHf"], w=[pscn, "PT"])
        po, pon = ps("hacc")
        for tb in range(NTB):
            for cl in range(2):
                cg = tb * 2 + cl
                par = (ti * NCH + cg) % 2
                mm(po[:, cg * 64:(cg + 1) * 64], itok[:, tb, hd * 128:(hd + 1) * 128], PT[:, tb, cl * 64:(cl + 1) * 64], [f"itok.{tb}", "PT"], [pon],
                   start=True, stop=False)
                mm(po[:, cg * 64:(cg + 1) * 64], Sb[:, hd, par, :], Qb[:, cg * 64:(cg + 1) * 64], ["Sb", "Qb"], [pon], start=False, stop=True)
                pd, pdn = ps("hps")
                rs = slice(cl * 64, (cl + 1) * 64)
                mm(pd[:, 0:128], KendT[rs, tb, :], itok[rs, tb, hd * 128:(hd + 1) * 128], ["KendT", f"itok.{tb}"], [pdn])
                V(lambda e, hd=hd, cg=cg, pd=pd: e.scalar_tensor_tensor(out=Sf[:, hd, :], in0=Sf[:, hd, :], scalar=sm[:, 2, cg:cg + 1], in1=pd[:, 0:128],
                                                                        op0=ALU.mult, op1=ALU.add), r=["sm"], w=[pdn, "Sf"])
                act(Sb[:, hd, 1 - par, :], Sf[:, hd, :], AF.Copy, ["Sf"], ["Sb"])
        eeb = ee.bitcast(BF16)[:, 0:TT]
        act(eeb, po[:, 0:TT], AF.Square, [], [pon, een])
        pss, pssn = ps("hps")
        mm(pss[:, 0:TT], c.onesb[:], eeb, ["onesb", een], [pssn])
        act(q[:], pss[:, 0:TT], AF.Ln, ["epsc"], [pssn, qn], scale=1.0 / 128.0, bias=c.epsc[:, 1:2])
        act(q[:], q[:], AF.Exp, [], [qn], scale=-0.5)
        V(lambda e, po=po: e.tensor_tensor(out=kk[:], in0=po[:, 0:TT], in1=q[:], op=ALU.mult), r=[qn], w=[pon, kkn_])
        V(lambda e, hd=hd: e.scalar_tensor_tensor(out=mixed[:, hd, :], in0=kk[:], scalar=PP(f"ang{e_}", hd), in1=gs[:], op0=ALU.mult, op1=ALU.mult),
          r=[kkn_, gsn, "ppt"], w=[f"{mxn}.{hd}"])


def rwkv_tile(c, ev, l, b, ti, mixed, mxn):
    S, V, A, G, PE, act, mm, ps, PP, sigm = c.S, c.V, c.A, c.G, c.PE, c.act, c.mm, c.ps, c.PP, c.sigm
    hb, wbig, proj_fm = c.hb, c.wbig, c.proj_fm
    e_ = l // 2
    tok0 = b * T + ti * TT
    wv = wbig[:].rearrange("p (dc n) -> p dc n", dc=8)
    C3 = lambda ap: ap.rearrange("p (c t) -> p c t", t=64)
    ONE, MONE = c.epsc[:, 3:4], c.epsc[:, 4:5]
    DK = float(np.exp(-0.5))

    def tr(out, in_, ident, r, w):
        return PE(lambda e: e.transpose(out=out, in_=in_, identity=ident), r, w)

    zc, zw, za, zv, hid, hidf, lwt, icl, vg, cr, raw = ev.zc, ev.zw, ev.za, ev.zv, ev.hid, ev.hidf, ev.lwt, ev.icl, ev.vg, ev.cr, ev.raw
    kh, cw, EW, rkp, ARz, Bt, Kt, Ns, NT0, LA, LB, Pq = ev.kh, ev.cw, ev.EW, ev.rkp, ev.ARz, ev.Bt, ev.Kt, ev.Ns, ev.NT0, ev.LA, ev.LB, ev.Pq
    Vtb, BKT, Yall, Ysq, st, Hs, Hb, sgb = ev.Vtb, ev.BKT, ev.Yall, ev.Ysq, ev.st, ev.Hs, ev.Hb, ev.sgb
    lora, vxo, vft = c.lora, c.vxo, c.vft
    lw1 = lora[:, 0:256].rearrange("p (j n) -> p j n", j=4)
    la1 = lora[:, 256:512].rearrange("p (j n) -> p j n", j=4)
    lv1 = lora[:, 512:640].rearrange("p (j n) -> p j n", j=4)
    lw2, la2, lv2 = lora[0:64, 640:1152], lora[0:64, 1152:1664], lora[0:32, 1664:2176]
    R0, K0, V0, Z0, G0 = 2048, 2560, 3072, 3584, 4096
    rg = ev.rg
    rgn = [f"rg{i}" for i in range(6)]
    if ti == 0:
        G(lambda e: e.memset(zc[:], 0.0), w=["zc"])
        G(lambda e: e.memset(cr[:], 0.0), w=["cr"])
        G(lambda e: e.memset(Hs[:], 0.0), w=[f"Hs.{j}" for j in range(4)])
        G(lambda e: e.memset(Hb[:], 0.0), w=[f"Hb.{j}" for j in range(4)])
    dz, dzn = rg[4][:, 0:TT], rgn[4]
    for j in range(4):
        zr, zrn = rg[j], rgn[j]
        G(lambda e, j=j, zr=zr: e.tensor_copy(out=zr[:, 0:1], in_=zc[:, j:j + 1]), r=["zc"], w=[zrn])
        pz, pzn = proj_fm(wv, Z0 + j * 128, "rproj")
        act(zr[:, 1:1 + TT], pz, AF.Copy, [], [pzn, zrn])
        G(lambda e, j=j, zr=zr: e.tensor_copy(out=zc[:, j:j + 1], in_=zr[:, TT:TT + 1]), r=[zrn], w=["zc"])
        V(lambda e, zr=zr: e.tensor_tensor(out=dz, in0=zr[:, 0:TT], in1=zr[:, 1:1 + TT], op=ALU.subtract), r=[zrn], w=[dzn])
        lst = [(zw, f"mu{e_}_3"), (za, f"mu{e_}_4")] + ([(zv, "vmu")] if e_ == 1 else [])
        for dst, mun in lst:
            V(lambda e, j=j, dst=dst, mun=mun, zr=zr: e.scalar_tensor_tensor(out=dst[:, j, :], in0=dz, scalar=PP(mun, j), in1=zr[:, 1:1 + TT],
                                                                           op0=ALU.mult, op1=ALU.add), r=[dzn, zrn, "ppt"], w=["zlerp"])
    ph, phn = ps("rmm")
    for j in range(4):
        mm(ph[0:64, 0:TT], lw1[:, j, :], zw[:, j, :], ["lora", "zlerp"], [phn], start=(j == 0), stop=(j == 3))
    sigm(hidf[:], "hidf", ph[0:64, 0:TT], [], w_extra=[phn], xscale=2.0)
    act(hid[:, 0, :], hidf[:], AF.Identity, ["hidf", "epsc"], ["hid"], scale=2.0, bias=MONE[0:64, :])
    ph, phn = ps("rmm")
    for j in range(4):
        mm(ph[0:64, 0:TT], la1[:, j, :], za[:, j, :], ["lora", "zlerp"], [phn], start=(j == 0), stop=(j == 3))
    act(hid[:, 1, :], ph[0:64, 0:TT], AF.Copy, [], [phn, "hid"])
    if e_ == 1:
        ph, phn = ps("rmm")
        for j in range(4):
            mm(ph[0:32, 0:TT], lv1[:, j, :], zv[:, j, :], ["lora", "zlerp"], [phn], start=(j == 0), stop=(j == 3))
        act(hid[0:32, 2, :], ph[0:32, 0:TT], AF.Copy, [], [phn, "hid"])

    rr, kx, t1, t2, t3, kkn = [g_[:, 0:TT] for g_ in rg]
    rrn, kxn, t1n, t2n, t3n, kknn = rgn
    wprev = ev.wprev
    if ti == 0:
        G(lambda e: e.memset(wprev[:], 1.0), w=["wprev"])

    def AU(f):
        return [S.record(f)]

    def OU(f):
        return S.record(f)

    def do_j(j):
        js = slice(j * 128, (j + 1) * 128)

        def lora_sig(dst, dn, w2, hrow, hid_ap, nb):
            def u():
                p_, pn_ = ps("rmm")
                mm(p_[:, 0:TT], w2[:, js], hid_ap, ["lora", "hid"], [pn_])
                act(dst[:], p_[:, 0:TT], AF.Exp, ["negb"], [pn_, dn], scale=-1.0, bias=nb)
            def rest():
                act(dst[:], dst[:], AF.Ln, ["epsc"], [dn], bias=ONE)
                act(dst[:], dst[:], AF.Exp, [], [dn], scale=-1.0)
            return AU(u) + OU(rest)

        T_lwt = lora_sig(lwt, "lwt", lw2, 64, hid[:, 0, :], c.NB_(f"w0{e_}", j))
        T_icl = lora_sig(icl, "icl", la2, 64, hid[:, 1, :], c.NB_(f"a0{e_}", j))
        T_vg = lora_sig(vg, "vg", lv2, 32, hid[0:32, 2, :], c.NB_("v0", j)) if e_ == 1 else []

        def g_unit():
            pg_, pgn_ = proj_fm(wv, G0 + j * 128, "rproj")
            act(t3, pg_, AF.Copy, [], [pgn_, t3n])
            act(sgb[:], pg_, AF.Exp, [], [pgn_, "sgb"], scale=-1.0)

        def g_rest():
            act(sgb[:], sgb[:], AF.Ln, ["epsc"], ["sgb"], bias=ONE)
            act(sgb[:], sgb[:], AF.Exp, [], ["sgb"], scale=-1.0)
            V(lambda e: e.tensor_tensor(out=sgb[:], in0=sgb[:], in1=t3, op=ALU.mult), r=[t3n], w=["sgb"])
        T_g = AU(g_unit) + OU(g_rest)

        T_rkv = []
        for kind, col0, dst, dn in ((0, R0, rr, rrn), (1, K0, kx, kxn), (2, V0, vxo[:], "vxo")):
            def pre(kind=kind):
                G(lambda e: e.tensor_copy(out=raw[:, 0:1], in_=cr[:, kind, j:j + 1]), r=["cr"], w=["raw"])
            def pu(col0=col0):
                pp_, ppn = proj_fm(wv, col0 + j * 128, "rproj")
                act(raw[:, 1:1 + TT], pp_, AF.Copy, [], [ppn, "raw"])
            def post(kind=kind, dst=dst, dn=dn):
                G(lambda e: e.tensor_copy(out=cr[:, kind, j:j + 1], in_=raw[:, TT:TT + 1]), r=["raw"], w=["cr"])
                V(lambda e: e.tensor_tensor(out=t1, in0=raw[:, 0:TT], in1=raw[:, 1:1 + TT], op=ALU.subtract), r=["raw"], w=[t1n])
                V(lambda e: e.scalar_tensor_tensor(out=dst, in0=t1, scalar=PP(f"mu{e_}_{kind}", j), in1=raw[:, 1:1 + TT],
                                                   op0=ALU.mult, op1=ALU.add), r=[t1n, "raw", "ppt"], w=[dn])
            T_rkv += OU(pre) + AU(pu) + OU(post)

        def vres():
            if e_ == 0:
                S.dma(c.vfd[js, tok0:tok0 + TT], vxo[:], "vxst", reads=["vxo"], writes=[f"vfd.{b}.{ti}.{j}"])
            else:
                S.dma(vft[:], c.vfd[js, tok0:tok0 + TT], "vft", reads=[f"vfd.{b}.{ti}.{j}"], writes=["vft"])
                V(lambda e: e.tensor_tensor(out=t1, in0=vft[:], in1=vxo[:], op=ALU.subtract), r=["vft", "vxo"], w=[t1n])
                V(lambda e: e.tensor_tensor(out=t1, in0=t1, in1=vg[:], op=ALU.mult), r=["vg"], w=[t1n])
                V(lambda e: e.tensor_tensor(out=vxo[:], in0=vxo[:], in1=t1, op=ALU.add), r=[t1n], w=["vxo"])
        S.replay(merge_threads([T_lwt, T_icl, T_vg, T_g, T_rkv]))
        vres()

        def kk1():
            V(lambda e: e.tensor_scalar_mul(out=t1, in0=kx, scalar1=PP(f"kk{e_}", j)), r=[kxn, "ppt"], w=[t1n])
            act(t2.bitcast(BF16)[:, 0:TT], t1, AF.Square, [t1n], [t2n])
        def kk2():
            pk, pkn = ps("rmm")
            mm(pk[:, 0:TT], c.bob[:], t2.bitcast(BF16)[:, 0:TT], ["bob", t2n], [pkn])
            V(lambda e, pk=pk: e.tensor_scalar_max(out=t2, in0=pk[:, 0:TT], scalar1=1e-24), w=[pkn, t2n])
        def kk3():
            act(t2, t2, AF.Ln, [], [t2n])
            act(t2, t2, AF.Exp, [], [t2n], scale=-0.5)
            V(lambda e: e.tensor_tensor(out=kkn, in0=t1, in1=t2, op=ALU.mult), r=[t1n, t2n], w=[kknn])
            V(lambda e: e.tensor_tensor(out=t2, in0=kkn, in1=icl[:], op=ALU.mult), r=[kknn, "icl"], w=[t2n])
        T_kk = OU(kk1) + AU(kk2) + OU(kk3)

        def khf():
            V(lambda e: e.tensor_scalar(out=t3, in0=icl[:], scalar1=PP(f"ka{e_}", j), scalar2=c.omka[:, e_, j:j + 1], op0=ALU.mult, op1=ALU.add),
              r=["icl", "ppt", "omka"], w=[t3n])
            V(lambda e: e.tensor_tensor(out=kh[:], in0=kx, in1=t3, op=ALU.mult), r=[kxn, t3n], w=["kh"])
            V(lambda e: e.scalar_tensor_tensor(out=rkp.bitcast(BF16)[:, 0:TT], in0=rr, scalar=PP(f"rk{e_}", j), in1=kh[:], op0=ALU.mult, op1=ALU.mult),
              r=[rrn, "kh", "ppt"], w=["rkp"])
        T_kh = OU(khf)

        def cwf():
            V(lambda e: e.tensor_tensor_scan(out=cw[:], data0=c.rm[:], data1=lwt[:], initial=0.0, op0=ALU.mult, op1=ALU.add), r=["rm", "lwt"], w=["cw"])
            act(EW[:], cw[:], AF.Exp, ["cw"], ["EW"], scale=-DK)
            V(lambda e: e.tensor_tensor(out=lwt[:], in0=cw[:], in1=lwt[:], op=ALU.subtract), r=["cw"], w=["lwt"])
            act(lwt[:], lwt[:], AF.Exp, [], ["lwt"], scale=-DK)
            act(cw[:], cw[:], AF.Exp, [], ["cw"], scale=DK)
        T_cw = OU(cwf)
        S.replay(merge_threads([T_kk, T_kh, T_cw]))
        for h in range(2):
            rs = slice(h * 64, (h + 1) * 64)
            V(lambda e, h=h, rs=rs: e.scalar_tensor_tensor(out=ARz[rs, h, :, 0, :], in0=C3(kkn[rs, :]), scalar=-1.0, in1=C3(lwt[rs, :]),
                                                          op0=ALU.mult, op1=ALU.mult), r=[kknn, "lwt"], w=["ARz"])
            V(lambda e, h=h, rs=rs: e.tensor_tensor(out=ARz[rs, h, :, 1, :], in0=C3(rr[rs, :]), in1=C3(EW[rs, :]), op=ALU.mult), r=[rrn, "EW"], w=["ARz"])
        V(lambda e: e.tensor_tensor(out=Bt[:], in0=t2, in1=cw[:], op=ALU.mult), r=[t2n, "cw"], w=["Bt"])
        V(lambda e: e.tensor_tensor(out=Kt[:], in0=kh[:], in1=cw[:], op=ALU.mult), r=["kh", "cw"], w=["Kt"])

        def chunk_thread(cg):
            cs = slice(cg * 64, (cg + 1) * 64)
            def u1():
                pv, pvn = ps("rmm")
                tr(pv[0:64, 0:128], vxo[:, cs], c.identf[:], ["vxo", "identf"], [pvn])
                act(Vtb[:, cg, :], pv[0:64, 0:128], AF.Copy, [], [pvn, f"Vtb.{cg}"])
            def u2():
                ptr, ptrn = ps("rmm")
                ptb = ptr[0:64, 0:128].bitcast(BF16)
                tr(ptb[:, 0:128], Bt[:, cs], c.identb[:], ["Bt", "identb"], [ptrn])
                tr(ptb[:, 128:256], Kt[:, cs], c.identb[:], ["Kt", "identb"], [ptrn])
                act(BKT[:, cg, :, :], ptb.rearrange("p (a n) -> p a n", a=2), AF.Copy, [], [ptrn, f"BKT.{cg}"])
            def u3():
                psn, psnn = ps("rmm")
                for h in range(2):
                    arz = ARz[:, h, cg, :, :].rearrange("p a n -> p (a n)")
                    mm(psn[0:64, h * 256:h * 256 + 128], Bt[:, cs], arz, ["Bt", "ARz"], [psnn])
                    mm(psn[0:64, h * 256 + 128:h * 256 + 256], Kt[:, cs], arz, ["Kt", "ARz"], [psnn])
                V(lambda e, psn=psn: e.tensor_tensor(out=Ns[:, cg, :, :].rearrange("p a n -> p (a n)"), in0=psn[0:64, :],
                                                     in1=c.mask512[:].rearrange("p a n -> p (a n)"), op=ALU.mult), r=["mask512"], w=[psnn, f"Ns.{cg}"])
            def u4():
                pt_, ptn = ps("rmm")
                for h in range(2):
                    mm(pt_[0:64, h * 64:(h + 1) * 64], ARz[:, h, cg, 0, :], Bt[:, cs], ["ARz", "Bt"], [ptn])
                V(lambda e, pt_=pt_: e.tensor_tensor(out=NT0[:, cg * 2:(cg + 1) * 2, :].rearrange("p a n -> p (a n)"), in0=pt_[0:64, 0:128],
                                                     in1=c.mL2[:].rearrange("p a n -> p (a n)"), op=ALU.mult), r=["mL2"], w=[ptn, f"NT0.{cg}"])
            return AU(u3) + AU(u4) + AU(u1) + AU(u2)
        S.replay(merge_threads([chunk_thread(cg) for cg in range(NCH)]))
        NsA = [f"Ns.{cg}" for cg in range(NCH)]
        NT0A = [f"NT0.{cg}" for cg in range(NCH)]
        Ns5 = Ns[:].rearrange("p c (h k) n -> p c h k n", h=2)
        N0v = Ns5[:, :, :, 0, :]
        V(lambda e: e.tensor_tensor(out=Pq[:].rearrange("p (c h) n -> p c h n", h=2), in0=N0v,
                                    in1=c.id8[:].rearrange("p (c h) n -> p c h n", h=2), op=ALU.add), r=NsA + ["id8"], w=["Pq"])

        def Nprev(lev, q_):
            if lev == 1:
                return N0v[:, q_ // 2, q_ % 2, :], f"Ns.{q_ // 2}"
            arr = LA if (lev - 1) % 2 == 1 else LB
            return arr[:, 0, q_, :], ("LA" if (lev - 1) % 2 == 1 else "LB")

        def NTprev(lev, q_):
            if lev == 1:
                return NT0[:, q_, :], f"NT0.{q_ // 2}"
            arr = LA if (lev - 1) % 2 == 1 else LB
            return arr[:, 1, q_, :], ("LA" if (lev - 1) % 2 == 1 else "LB")

        for lev in range(1, 6):
            dst = LA if lev % 2 == 1 else LB
            dstn = "LA" if lev % 2 == 1 else "LB"
            need_n = lev < 5
            if need_n:
                pN, pNn = ps("rmm")
                for q_ in range(8):
                    n_ap, n_nm = Nprev(lev, q_)
                    nt_ap, nt_nm = NTprev(lev, q_)
                    mm(pN[0:64, q_ * 64:(q_ + 1) * 64], nt_ap, n_ap, [n_nm, nt_nm], [pNn])
            pNT, pNTn = ps("rmm")
            for q_ in range(8):
                n_ap, n_nm = Nprev(lev, q_)
                nt_ap, nt_nm = NTprev(lev, q_)
                mm(pNT[0:64, q_ * 64:(q_ + 1) * 64], n_ap, nt_ap, [n_nm, nt_nm], [pNTn])
            if need_n:
                act(dst[:, 0, :, :].rearrange("p a n -> p (a n)"), pN[0:64, :], AF.Copy, [], [pNn, dstn])
            V(lambda e, dst=dst, pNT=pNT: e.tensor_copy(out=dst[:, 1, :, :].rearrange("p a n -> p (a n)"), in_=pNT[0:64, :]), w=[pNTn, dstn])
            pP, pPn = ps("rmm")
            for q_ in range(8):
                mm(pP[0:64, q_ * 64:(q_ + 1) * 64], dst[:, 1, q_, :], Pq[:, q_, :], [dstn, "Pq"], [pPn])
            V(lambda e, pP=pP: e.tensor_tensor(out=Pq[:].rearrange("p a n -> p (a n)"), in0=Pq[:].rearrange("p a n -> p (a n)"), in1=pP[0:64, :], op=ALU.add),
              w=[pPn, "Pq"])
        Hbj = Hb[:, j, :]
        for cg in range(NCH):
            pc, pcn = ps("racc")
            Xs, Us = ev.Xsb[cg % 2], ev.Usb[cg % 2]
            xn, un = f"Xsb{cg % 2}", f"Usb{cg % 2}"
            nsn, vtn, bkn = f"Ns.{cg}", f"Vtb.{cg}", f"BKT.{cg}"
            for h in range(2):
                hs = slice(h * 64, (h + 1) * 64)
                mm(pc[0:64, hs], ARz[:, h, cg, 0, :], Hbj, ["ARz", f"Hb.{j}"], [pcn], start=True, stop=False)
                mm(pc[0:64, hs], Ns[:, cg, h * 4 + 2, :], Vtb[:, cg, hs], [nsn, vtn], [pcn], start=False, stop=True)
            act(Xs[:], pc[0:64, 0:128], AF.Copy, [], [pcn, xn])
            for h in range(2):
                hs = slice(h * 64, (h + 1) * 64)
                mm(pc[0:64, 128 + h * 64:128 + (h + 1) * 64], Pq[:, cg * 2 + h, :], Xs[:, hs], ["Pq", xn], [pcn])
            act(Us[:], pc[0:64, 128:256], AF.Copy, [], [pcn, un])
            for h in range(2):
                hs = slice(h * 64, (h + 1) * 64)
                ys = slice(256 + h * 64, 256 + (h + 1) * 64)
                mm(pc[0:64, ys], ARz[:, h, cg, 1, :], Hbj, ["ARz", f"Hb.{j}"], [pcn], start=True, stop=False)
                mm(pc[0:64, ys], Ns[:, cg, h * 4 + 1, :], Us[:, hs], [nsn, un], [pcn], start=False, stop=False)
                mm(pc[0:64, ys], Ns[:, cg, h * 4 + 3, :], Vtb[:, cg, hs], [nsn, vtn], [pcn], start=False, stop=True)
            mm(pc[:, 384:512], BKT[:, cg, 0, :], Us[:], [bkn, un], [pcn], start=True, stop=False)
            mm(pc[:, 384:512], BKT[:, cg, 1, :], Vtb[:, cg, :], [bkn, vtn], [pcn], start=False, stop=True)
            wsc = wprev[:, j:j + 1] if cg == 0 else EW[:, cg * 64 - 1:cg * 64]
            for h in range(2):
                rs = slice(h * 64, (h + 1) * 64)
                V(lambda e, h=h, rs=rs, pc=pc, wsc=wsc: e.scalar_tensor_tensor(out=Hs[rs, j, :], in0=Hs[rs, j, :], scalar=wsc[rs, :],
                                                                              in1=pc[rs, 384 + h * 64:384 + (h + 1) * 64], op0=ALU.mult, op1=ALU.add),
                  r=["EW", "wprev"], w=[pcn, f"Hs.{j}"])
            act(Hb[:, j, :], Hs[:, j, :], AF.Copy, [f"Hs.{j}", "EW"], [f"Hb.{j}"], scale=EW[:, cg * 64 + 63:cg * 64 + 64])
            act(Yall[:, cg, :], pc[0:64, 256:384], AF.Copy, [], [pcn, "Yall"])
        V(lambda e, j=j: e.tensor_copy(out=wprev[:, j:j + 1], in_=EW[:, TT - 1:TT]), r=["EW"], w=["wprev"])
        Y8 = Yall[:].rearrange("p c (h n) -> p (c h) n", h=2)
        Q8 = Ysq[:].rearrange("p c (h n) -> p (c h) n", h=2)
        V(lambda e: e.tensor_reduce(out=st[:, 0, :], in_=Y8, axis=AX.X, op=ALU.add), r=["Yall"], w=["st"])
        act(Ysq[:], Yall[:], AF.Square, ["Yall"], ["Ysq"])
        V(lambda e: e.tensor_reduce(out=st[:, 1, :], in_=Q8, axis=AX.X, op=ALU.add), r=["Ysq"], w=["st"])
        V(lambda e: e.tensor_scalar_mul(out=st[:, 0, :], in0=st[:, 0, :], scalar1=1.0 / 64.0), w=["st"])
        V(lambda e: e.tensor_tensor(out=st[:, 2, :], in0=st[:, 0, :], in1=st[:, 0, :], op=ALU.mult), w=["st"])
        V(lambda e: e.scalar_tensor_tensor(out=st[:, 1, :], in0=st[:, 1, :], scalar=1.0 / 64.0, in1=st[:, 2, :], op0=ALU.mult, op1=ALU.subtract), w=["st"])
        act(st[:, 1, :], st[:, 1, :], AF.Ln, ["epsc"], ["st"], bias=c.epsc[0:64, 2:3])
        act(st[:, 1, :], st[:, 1, :], AF.Exp, [], ["st"], scale=-0.5)
        V(lambda e: e.tensor_tensor(out=Q8, in0=Y8, in1=st[:, 0, :].unsqueeze(2).to_broadcast([64, 8, 64]), op=ALU.subtract), r=["Yall", "st"], w=["Ysq"])
        V(lambda e: e.tensor_tensor(out=Q8, in0=Q8, in1=st[:, 1, :].unsqueeze(2).to_broadcast([64, 8, 64]), op=ALU.mult), r=["st"], w=["Ysq"])
        pf_, pfn_ = ps("rmm")
        for cg in range(NCH):
            tr(pf_[:, cg * 64:(cg + 1) * 64], Ysq[:, cg, :], c.identf[0:64, 0:64], ["Ysq", "identf"], [pfn_])
        act(t1, pf_[:, 0:TT], AF.Identity, ["ppt"], [pfn_, t1n], scale=PP(f"gng{e_}", j), bias=PP(f"gnb{e_}", j))
        pr_, prn_ = ps("rmm")
        mm(pr_[:, 0:TT], c.bob[:], rkp.bitcast(BF16)[:, 0:TT], ["bob", "rkp"], [prn_])
        V(lambda e, pr_=pr_: e.tensor_tensor(out=t2, in0=pr_[:, 0:TT], in1=vxo[:], op=ALU.mult), r=["vxo"], w=[prn_, t2n])
        V(lambda e: e.tensor_tensor(out=t1, in0=t1, in1=t2, op=ALU.add), r=[t2n], w=[t1n])
        V(lambda e, j=j: e.tensor_tensor(out=mixed[:, 4 + j, :], in0=t1, in1=sgb[:], op=ALU.mult), r=[t1n, "sgb"], w=[f"{mxn}.{4 + j}"])

    for j in range(4):
        do_j(j)
```
